# Optimizing a Trainium2 kernel written in Bass

```python
import math
import jax, jax.numpy as jnp
from jax import lax
import numpy as np

D_MODEL = 1024
BATCH = 32
SEQ = 256
DEPTH = 2
DEC_BATCH = 2
DEC_SEQ = 4096
PAST_LEN = 512

GRID_W = 64
Q_BLOCK = 128
RMS_EPS = 1e-6
ROPE_THETA = 10000.0
MLA_HEADS = 8
MLA_NOPE = 64
MLA_ROPE = 32
MLA_V = 64
Q_LORA = 256
KV_LORA = 128
GQA_HEADS = 8
GQA_KV_HEADS = 2
GQA_HEAD_DIM = 64
POOL_WINDOWS = (2, 4, 8, 16)
POOL_GROUP = 128
POOL_WIDTH = POOL_GROUP * len(POOL_WINDOWS)
D_FF = 2816
N_BRANCH = 3
MLA_OUT = MLA_HEADS * MLA_V
GQA_OUT = GQA_HEADS * GQA_HEAD_DIM
IN_SPLITS = (Q_LORA, KV_LORA + MLA_ROPE, GQA_HEADS * GQA_HEAD_DIM, GQA_KV_HEADS * GQA_HEAD_DIM,
             GQA_KV_HEADS * GQA_HEAD_DIM, POOL_WIDTH, N_BRANCH * D_MODEL)
IN_COLS = Q_LORA + KV_LORA + MLA_ROPE + (GQA_HEADS + 2 * GQA_KV_HEADS) * GQA_HEAD_DIM + POOL_WIDTH + N_BRANCH * D_MODEL

kernel_name = "hybrid_diffusion_mla_gqa_pool_convffn_step"


def rms_norm(x, g):
    xf = x.astype(jnp.float32)
    xf = xf * lax.rsqrt(jnp.mean(xf * xf, axis=-1, keepdims=True) + RMS_EPS)
    return xf.astype(x.dtype) * g


def split_cols(z):
    outs = []
    start = 0
    for w in IN_SPLITS:
        outs.append(z[..., start:start + w])
        start += w
    return outs


def axial_angles(n_tokens, dim_axis):
    n_rows = n_tokens // GRID_W
    row = jnp.repeat(jnp.arange(n_rows), GRID_W).astype(jnp.float32)
    col = jnp.tile(jnp.arange(GRID_W), n_rows).astype(jnp.float32)
    freqs = ROPE_THETA ** (-jnp.arange(0, dim_axis, 2, dtype=jnp.float32) / dim_axis)
    return row[:, None] * freqs[None, :], col[:, None] * freqs[None, :]


def rope_rotate(x, ang):
    half = x.shape[-1] // 2
    x1, x2 = x[..., :half], x[..., half:]
    cos = jnp.cos(ang)[None, :, None, :].astype(x.dtype)
    sin = jnp.sin(ang)[None, :, None, :].astype(x.dtype)
    return jnp.concatenate([x1 * cos - x2 * sin, x1 * sin + x2 * cos], axis=-1)


def axial_rope(x):
    T, d = x.shape[1], x.shape[-1]
    ang_row, ang_col = axial_angles(T, d // 2)
    h = d // 2
    return jnp.concatenate([rope_rotate(x[..., :h], ang_row), rope_rotate(x[..., h:], ang_col)], axis=-1)


def block_attention(q, k, v):
    B, T, H, D = q.shape
    G = k.shape[2]
    R = H // G
    Dv = v.shape[-1]
    scale = D ** -0.5
    qb = q.reshape(B, T // Q_BLOCK, Q_BLOCK, G, R, D).transpose(1, 0, 2, 3, 4, 5)

    def one_block(qi):
        s = jnp.einsum('bqgrd,bsgd->bgrqs', qi, k).astype(jnp.float32) * scale
        pr = jax.nn.softmax(s, axis=-1).astype(v.dtype)
        return jnp.einsum('bgrqs,bsgv->bqgrv', pr, v)

    o = lax.map(one_block, qb)
    return o.transpose(1, 0, 2, 3, 4, 5).reshape(B, T, H * Dv)


def pool_mixer(u, w_pool, pool_scale):
    B, T, _ = u.shape
    uf = u.astype(jnp.float32)
    cs = jnp.concatenate([jnp.zeros((B, 1, POOL_WIDTH), jnp.float32), jnp.cumsum(uf, axis=1)], axis=1)
    t = jnp.arange(T)
    pooled = []
    for g, w in enumerate(POOL_WINDOWS):
        lo = jnp.clip(t - w // 2, 0, T)
        hi = jnp.clip(t - w // 2 + w, 0, T)
        seg = cs[:, :, g * POOL_GROUP:(g + 1) * POOL_GROUP]
        cnt = (hi - lo).astype(jnp.float32)[None, :, None]
        pooled.append((seg[:, hi] - seg[:, lo]) / cnt)
    pooled = jnp.concatenate(pooled, axis=-1) - uf
    pooled = pooled.astype(u.dtype).reshape(B, T, len(POOL_WINDOWS), POOL_GROUP)
    out = jnp.einsum('btgc,gcd->btgd', pooled, w_pool).reshape(B, T, POOL_WIDTH)
    return out * pool_scale


def conv_ffn(h, w_up, conv_w, conv_b, w_down):
    u = h @ w_up
    g, val = u[..., :D_FF], u[..., D_FF:]
    gp = jnp.pad(g, ((0, 0), (1, 1), (0, 0)))
    g = gp[:, :-2] * conv_w[0] + gp[:, 1:-1] * conv_w[1] + gp[:, 2:] * conv_w[2] + conv_b
    return (jax.nn.silu(g) * val) @ w_down


def trunk_layer(x, mod, p, cache=None):
    B, T, _ = x.shape
    sh1, sc1, g1, sh2, sc2, g2 = jnp.split(mod, 6, axis=-1)
    h = rms_norm(x, p['g_norm_mix']) * (1 + sc1) + sh1
    q_a, kv_a, gq, gk, gv, pool_in, gate_logits = split_cols(h @ p['w_in'])
    q_mla = (rms_norm(q_a, p['g_q_a']) @ p['w_q_b']).reshape(B, T, MLA_HEADS, MLA_NOPE + MLA_ROPE)
    ckv = rms_norm(kv_a[..., :KV_LORA], p['g_kv_a'])
    krope = kv_a[..., KV_LORA:]
    q_gqa = rms_norm(gq.reshape(B, T, GQA_HEADS, GQA_HEAD_DIM), p['g_q_gqa'])
    k_gqa = rms_norm(gk.reshape(B, T, GQA_KV_HEADS, GQA_HEAD_DIM), p['g_k_gqa'])
    v_gqa = gv.reshape(B, T, GQA_KV_HEADS, GQA_HEAD_DIM)
    if cache is None:
        new_state = (ckv, krope, k_gqa, v_gqa)
        ckv_all, krope_all, k_all, v_all = ckv, krope, k_gqa, v_gqa
    else:
        c_ckv, c_krope, c_k, c_v = cache
        q_mla = jnp.concatenate([q_mla[..., :MLA_NOPE], axial_rope(q_mla[..., MLA_NOPE:])], axis=-1)
        krope = axial_rope(krope[:, :, None, :])[:, :, 0, :]
        q_gqa = axial_rope(q_gqa)
        k_gqa = axial_rope(k_gqa)
        ckv_all = jnp.concatenate([ckv, c_ckv], axis=1)
        krope_all = jnp.concatenate([krope, c_krope], axis=1)
        k_all = jnp.concatenate([k_gqa, c_k], axis=1)
        v_all = jnp.concatenate([v_gqa, c_v], axis=1)
        new_state = None
    S = ckv_all.shape[1]
    kv = (ckv_all @ p['w_kv_b']).reshape(B, S, MLA_HEADS, MLA_NOPE + MLA_V)
    k_mla = jnp.concatenate([kv[..., :MLA_NOPE],
                             jnp.broadcast_to(krope_all[:, :, None, :], (B, S, MLA_HEADS, MLA_ROPE))], axis=-1)
    a_out = block_attention(q_mla, k_mla, kv[..., MLA_NOPE:])
    b_out = block_attention(q_gqa, k_all, v_all)
    c_out = pool_mixer(pool_in, p['w_pool'], p['pool_scale'])
    gates = jax.nn.sigmoid(gate_logits)
    ga, gb, gc = jnp.split(gates, N_BRANCH, axis=-1)
    merged = ga * (a_out @ p['w_br_a']) + gb * (b_out @ p['w_br_b']) + gc * (c_out @ p['w_br_c'])
    x = x + g1 * (merged @ p['w_out'])
    h2 = rms_norm(x, p['g_norm_ffn']) * (1 + sc2) + sh2
    x = x + g2 * conv_ffn(h2, p['w_up'], p['conv_w'], p['conv_b'], p['w_down'])
    return x, new_state


def setup_inputs(seed: int = 0) -> dict:
    key = jax.random.key(seed)
    ks = jax.random.split(key, 40)
    f32 = jnp.float32

    def nrm(k, shape, scale=1.0):
        return jax.random.normal(k, shape, f32) * scale

    def gain(k, shape):
        return 1.0 + 0.1 * jax.random.normal(k, shape, f32)

    L = DEPTH
    return {
        'x_prompt': nrm(ks[0], (BATCH, SEQ, D_MODEL)),
        'x_sample': nrm(ks[1], (DEC_BATCH, DEC_SEQ, D_MODEL)),
        'c': nrm(ks[2], (DEC_BATCH, D_MODEL)),
        'cache_mla_ckv': nrm(ks[3], (DEC_BATCH, L, PAST_LEN, KV_LORA)),
        'cache_mla_krope': nrm(ks[4], (DEC_BATCH, L, PAST_LEN, MLA_ROPE)),
        'cache_gqa_k': nrm(ks[5], (DEC_BATCH, L, PAST_LEN, GQA_KV_HEADS, GQA_HEAD_DIM)),
        'cache_gqa_v': nrm(ks[6], (DEC_BATCH, L, PAST_LEN, GQA_KV_HEADS, GQA_HEAD_DIM)),
        'c_ctx': nrm(ks[7], (D_MODEL,)),
        'w_ada': nrm(ks[8], (L, D_MODEL, 6 * D_MODEL), 0.5 * D_MODEL ** -0.5),
        'b_ada': nrm(ks[9], (L, 6 * D_MODEL), 0.01),
        'g_norm_mix': gain(ks[10], (L, D_MODEL)),
        'w_in': nrm(ks[11], (L, D_MODEL, IN_COLS), D_MODEL ** -0.5),
        'g_q_a': gain(ks[12], (L, Q_LORA)),
        'w_q_b': nrm(ks[13], (L, Q_LORA, MLA_HEADS * (MLA_NOPE + MLA_ROPE)), Q_LORA ** -0.5),
        'g_kv_a': gain(ks[14], (L, KV_LORA)),
        'w_kv_b': nrm(ks[15], (L, KV_LORA, MLA_HEADS * (MLA_NOPE + MLA_V)), KV_LORA ** -0.5),
        'g_q_gqa': gain(ks[16], (L, GQA_HEAD_DIM)),
        'g_k_gqa': gain(ks[17], (L, GQA_HEAD_DIM)),
        'w_pool': nrm(ks[18], (L, len(POOL_WINDOWS), POOL_GROUP, POOL_GROUP), POOL_GROUP ** -0.5),
        'pool_scale': gain(ks[19], (L, POOL_WIDTH)),
        'w_br_a': nrm(ks[20], (L, MLA_OUT, D_MODEL), MLA_OUT ** -0.5),
        'w_br_b': nrm(ks[21], (L, GQA_OUT, D_MODEL), GQA_OUT ** -0.5),
        'w_br_c': nrm(ks[22], (L, POOL_WIDTH, D_MODEL), POOL_WIDTH ** -0.5),
        'w_out': nrm(ks[23], (L, D_MODEL, D_MODEL), D_MODEL ** -0.5),
        'g_norm_ffn': gain(ks[24], (L, D_MODEL)),
        'w_up': nrm(ks[25], (L, D_MODEL, 2 * D_FF), D_MODEL ** -0.5),
        'conv_w': nrm(ks[26], (L, 3, D_FF), 3 ** -0.5),
        'conv_b': nrm(ks[27], (L, D_FF), 0.01),
        'w_down': nrm(ks[28], (L, D_FF, D_MODEL), D_FF ** -0.5),
        'g_final': gain(ks[29], (D_MODEL,)),
    }


def reference(x_prompt, x_sample, c, cache_mla_ckv, cache_mla_krope, cache_gqa_k, cache_gqa_v,
              c_ctx, w_ada, b_ada, g_norm_mix, w_in, g_q_a, w_q_b, g_kv_a, w_kv_b, g_q_gqa, g_k_gqa,
              w_pool, pool_scale, w_br_a, w_br_b, w_br_c, w_out, g_norm_ffn, w_up, conv_w, conv_b,
              w_down, g_final):
    xp = x_prompt
    xs = x_sample
    st_ckv, st_krope, st_k, st_v = [], [], [], []
    for l in range(DEPTH):
        p = {
            'g_norm_mix': g_norm_mix[l], 'w_in': w_in[l], 'g_q_a': g_q_a[l], 'w_q_b': w_q_b[l],
            'g_kv_a': g_kv_a[l], 'w_kv_b': w_kv_b[l], 'g_q_gqa': g_q_gqa[l], 'g_k_gqa': g_k_gqa[l],
            'w_pool': w_pool[l], 'pool_scale': pool_scale[l], 'w_br_a': w_br_a[l], 'w_br_b': w_br_b[l],
            'w_br_c': w_br_c[l], 'w_out': w_out[l], 'g_norm_ffn': g_norm_ffn[l], 'w_up': w_up[l],
            'conv_w': conv_w[l], 'conv_b': conv_b[l], 'w_down': w_down[l],
        }
        mod_ctx = (jax.nn.silu(c_ctx) @ w_ada[l] + b_ada[l])[None, None, :]
        xp, st = trunk_layer(xp, mod_ctx, p)
        st_ckv.append(st[0])
        st_krope.append(st[1])
        st_k.append(st[2])
        st_v.append(st[3])
        mod_lat = (jax.nn.silu(c) @ w_ada[l] + b_ada[l])[:, None, :]
        xs, _ = trunk_layer(xs, mod_lat, p,
                            cache=(cache_mla_ckv[:, l], cache_mla_krope[:, l], cache_gqa_k[:, l], cache_gqa_v[:, l]))
    y_prompt = rms_norm(xp, g_final)
    y_sample = rms_norm(xs, g_final)
    new_mla_ckv = jnp.stack(st_ckv, axis=1)
    new_mla_krope = jnp.stack(st_krope, axis=1)
    new_gqa_k = jnp.stack(st_k, axis=1)
    new_gqa_v = jnp.stack(st_v, axis=1)
    return (y_prompt, y_sample, new_mla_ckv, new_mla_krope, new_gqa_k, new_gqa_v)
```

```python
import contextlib
import os
import numpy as np
import concourse.bass as bass
import concourse.mybir as mybir
from concourse.bass_utils import run_bass_kernel_spmd

F32 = mybir.dt.float32
BF16 = mybir.dt.bfloat16
ALU = mybir.AluOpType
AF = mybir.ActivationFunctionType

D = 1024
KC = 8
L = 2
TP = 1024
TS = 1152
HALO = 64
OWN = 1024
S_ALL = 4608
NKB_S = 36
D_FF = 2816
NFC = 22
EPS = 1e-6
IN_COLS = 4768
C_QA, C_KV, C_KR, C_GQ, C_GK, C_GV, C_PO, C_GT = 0, 256, 384, 416, 928, 1056, 1184, 1696
NV = 176
V_GMIX, V_GFFN, V_BADA, V_GQA, V_GKVA, V_GQ, V_GQP, V_GK, V_GKP, V_PSC, V_CW, V_CB, V_GF = \
    0, 8, 16, 64, 66, 67, 68, 69, 70, 71, 75, 141, 163
SBUF_BASE = 16512
SBUF_END = 229376 - 128


class Op:
    __slots__ = ("eng", "fn", "deps", "signal", "sigval", "dma_sem", "dma_val", "idx")


class Obj:
    __slots__ = ("last_w", "readers")

    def __init__(self):
        self.last_w = None
        self.readers = []


COMPUTE = ("pe", "act", "dve", "pool")
QUEUES = ("sp",)


class Prog:
    def __init__(self, nc):
        self.nc = nc
        self.ops = {e: [] for e in COMPUTE + QUEUES}
        self.objs = {}
        self.n = 0
        self.dma_counts = {}
        self.free_banks = list(range(8))
        self.dry = False
        self.fam_ctr = {}
        self.last_on_sem = {}

    def o(self, key):
        ob = self.objs.get(key)
        if ob is None:
            ob = self.objs[key] = Obj()
        return ob

    def add(self, eng, fn, r=(), w=(), dma_sem=None, grp=False):
        if self.dry:
            return None
        fam_prev = None
        if isinstance(dma_sem, tuple):
            fam, nmem = dma_sem
            i = self.fam_ctr.get(fam, 0)
            self.fam_ctr[fam] = i + 1
            dma_sem = f"{fam}{i % nmem}"
            fam_prev = self.last_on_sem.get(dma_sem)
        op = Op()
        op.eng = eng
        op.fn = fn
        op.signal = False
        op.sigval = None
        op.dma_sem = dma_sem
        op.dma_val = None
        op.idx = self.n
        self.n += 1
        deps = {}
        is_dma = dma_sem is not None
        for k in r:
            ob = self.o(k)
            d = ob.last_w
            if d is not None:
                deps[d.idx] = d
            if isinstance(k, tuple) and k[0] == "B":
                for d in ob.readers:
                    if d.eng != eng:
                        deps[d.idx] = d
        for k in w:
            ob = self.o(k)
            d = ob.last_w
            if d is not None:
                deps[d.idx] = d
            for d in ob.readers:
                deps[d.idx] = d
        if grp:
            deps = {i: d for i, d in deps.items() if d.dma_sem != dma_sem}
        if fam_prev is not None:
            deps[fam_prev.idx] = fam_prev
        if eng == "pe":
            deps = {i: d for i, d in deps.items() if not (d.eng == "pe" and d.dma_sem is None)}
        op.deps = list(deps.values())
        for d in op.deps:
            d.signal = True
        for k in r:
            self.o(k).readers.append(op)
        for k in w:
            ob = self.o(k)
            ob.last_w = op
            ob.readers = []
        if is_dma:
            c = self.dma_counts.get(dma_sem, 0) + (1 if dma_sem.startswith("cc") else 16)
            self.dma_counts[dma_sem] = c
            op.dma_val = c
            self.last_on_sem[dma_sem] = op
        self.ops[eng].append(op)
        return op

    def bank(self):
        if self.dry:
            return 0
        assert self.free_banks, "out of PSUM banks"
        return self.free_banks.pop(0)

    def free(self, b):
        if self.dry:
            return
        self.free_banks.append(b)

    def emit(self, block, sems, dma_sems):
        for e in COMPUTE:
            c = 0
            for op in self.ops[e]:
                if op.signal and op.dma_sem is None:
                    c += 1
                    op.sigval = c
        eng_fn = {"pe": block.tensor, "act": block.scalar, "dve": block.vector, "pool": block.gpsimd,
                  "sp": block.sync}
        for e in COMPUTE + QUEUES:
            ops = self.ops[e]

            def body(eng, ops=ops, e=e):
                waited = {}
                for op in ops:
                    need = {}
                    for d in op.deps:
                        if d.dma_sem is not None:
                            key, val = ("d", d.dma_sem), d.dma_val
                        else:
                            key, val = ("c", d.eng), d.sigval
                        if need.get(key, 0) < val:
                            need[key] = val
                    for key, val in need.items():
                        if waited.get(key, 0) >= val:
                            continue
                        waited[key] = val
                        sem = dma_sems[key[1]] if key[0] == "d" else sems[key[1]]
                        eng.wait_ge(sem, val)
                    ins = op.fn(eng)
                    if op.dma_sem is not None and op.dma_sem.startswith("cc"):
                        ins.then_inc(dma_sems[op.dma_sem])
                    elif op.dma_sem is not None:
                        ins.then_inc(dma_sems[op.dma_sem], 16)
                    elif op.signal:
                        ins.then_inc(sems[e], 1)
                if e == "sp":
                    for name, cnt in self.dma_counts.items():
                        eng.wait_ge(dma_sems[name], cnt)
            eng_fn[e](body)


class Group:
    def __init__(self, name, T, tiles, nseg, seglen, sample):
        self.name, self.T, self.tiles, self.nseg, self.seglen, self.sample = name, T, tiles, nseg, seglen, sample


GP = Group("p", TP, [(0, 256), (256, 256), (512, 256), (768, 256)], 4, 256, False)
GS = Group("s", TS, [(0, 384), (384, 384), (768, 384)], 1, TS, True)


def MM(P, out, lhsT, rhs, start, stop, r, w):
    return P.add("pe", lambda e: e.matmul(out, lhsT=lhsT, rhs=rhs, start=start, stop=stop), r, w)


def ACT(P, out, in_, func, r, w, bias=None, scale=None):
    kw = {}
    if bias is not None:
        kw["bias"] = bias
    if scale is not None:
        kw["scale"] = scale
    return P.add("act", lambda e: e.activation(out=out, in_=in_, func=func, **kw), r, w)


def TT(P, eng, out, in0, in1, op, r, w):
    return P.add(eng, lambda e: e.tensor_tensor(out=out, in0=in0, in1=in1, op=op), r, w)


def STT(P, eng, out, in0, scalar, in1, op0, op1, r, w):
    return P.add(eng, lambda e: e.scalar_tensor_tensor(out=out, in0=in0, scalar=scalar, in1=in1,
                                                       op0=op0, op1=op1), r, w)


def CP(P, eng, out, in_, r, w):
    if eng == "act":
        return P.add("act", lambda e: e.activation(out=out, in_=in_, func=AF.Copy), r, w)
    return P.add(eng, lambda e: e.tensor_copy(out=out, in_=in_), r, w)


def MEMSET(P, eng, ap, val, w):
    return P.add(eng, lambda e: e.memset(ap, val), (), w)


def RECIP(P, out, in_, r, w):
    return P.add("dve", lambda e: e.reciprocal(out=out, in_=in_), r, w)


def DMA(P, q, out, in_, sem, r, w, grp=False):
    if isinstance(sem, tuple) and sem[0] == "out" and 'o' in os.environ.get('K_SKIP', ''):
        return None
    return P.add(q, lambda e: e.dma_start(out=out, in_=in_), r, w, dma_sem=sem, grp=grp)


_UID = [0]


class Frame:
    def __init__(self, nc, base, limit, tag):
        self.nc, self.cur, self.limit, self.tag = nc, base, limit, tag
        self.n = 0

    def alloc(self, name, shape, dtype):
        nbytes = int(np.prod(shape[1:])) * (4 if dtype == F32 else 2)
        nbytes = (nbytes + 63) // 64 * 64
        off = self.cur
        self.cur += nbytes
        assert self.cur <= self.limit, f"SBUF frame {self.tag} overflow at {name}: {self.cur} > {self.limit}"
        self.n += 1
        _UID[0] += 1
        return self.nc.alloc_sbuf_tensor_at(f"{self.tag}_{name}_{off}_{_UID[0]}", list(shape), dtype, offset=off)


def build_program(only=None):
    if os.environ.get('K_LAZY') == '1' and only is None:
        _, P0 = build_program(only=())
        return build_program(only=tuple(P0.used_io['in'] + P0.used_io['out'] + ['y_p']))
    STOP = int(os.environ.get('K_STOP', '1000'))
    SUB = int(os.environ.get('K_SUB', '1000'))
    SKIP = set(os.environ.get('K_SKIP', ''))
    nc = bass.Bass("TRN2", target_bir_lowering=False)

    used_io = {"in": [], "out": []}

    class Lz:
        def __init__(self, name, shape, kind):
            self.name, self.shape, self.kind, self._ap = name, list(shape), kind, None

        def get(self):
            if self._ap is None:
                self._ap = nc.dram_tensor(self.name, self.shape, F32,
                                          kind="ExternalInput" if self.kind == "in" else "ExternalOutput").ap()
                used_io[self.kind].append(self.name)
            return self._ap

        def __getitem__(self, k):
            return self.get()[k]

        def rearrange(self, *a, **kw):
            return self.get().rearrange(*a, **kw)

    LAZY = only == ()

    def din(name, shape, dt=F32):
        z = Lz(name, shape, "in")
        if not LAZY and (only is None or name in only):
            z.get()
        return z

    def dout(name, shape, dt=F32):
        z = Lz(name, shape, "out")
        if not LAZY and (only is None or name in only):
            z.get()
        return z

    xp_d = din("xp", [128, KC, TP])
    xs_d = din("xs", [128, KC, TS])
    cvec_d = din("cvec", [128, KC, 2])
    cckv_d = din("c_ckv", [L, 128, 512])
    ckr_d = din("c_krope", [L, 32, 512])
    ck_d = din("c_k", [L, 2, 64, 512])
    cv_d = din("c_v", [L, 128, 4, 128])
    wada_d = din("w_ada", [L, D, 6 * D])
    win_d = din("w_in", [L, D, IN_COLS])
    wqb_d = din("w_q_b", [L, 256, 768])
    wkvb_d = din("w_kv_b", [L, 128, 1024])
    wpool_d = din("w_pool", [L, 4, 128, 128])
    wbr_d = [din("w_br_a", [L, 512, D]), din("w_br_b", [L, 512, D]), din("w_br_c", [L, 512, D])]
    wout_d = din("w_out", [L, D, D])
    wup_d = din("w_up", [L, D, 2 * D_FF])
    wdn_d = din("w_down", [L, D_FF, D])
    vecs_d = din("vecs", [128, L, NV])
    rope64_d = din("rope64", [128, 2, TS])
    rope32_d = din("rope32", [128, 2, TS])
    mask_d = din("mask", [128, TS])
    ptab_d = {"s": din("ptab_s", [128, 4, TS]), "p": din("ptab_p", [128, 4, TP])}
    sel_d = din("sel", [32, 96])

    yp_d = dout("y_p", [128, KC, TP])
    ys_d = dout("y_s", [128, KC, OWN])
    ockv_d = dout("o_ckv", [L, 128, TP])
    okr_d = dout("o_krope", [L, 32, TP])
    ok_d = dout("o_k", [L, 2, 64, TP])
    ov_d = dout("o_v", [L, 128, 8, 128])

    xp_scr = nc.dram_tensor("xp_scr", [128, KC, TP], F32).ap()
    send_t = [nc.dram_tensor(f"send{l}", [416, OWN], BF16) for l in range(L)]
    recv_t = [nc.dram_tensor(f"recv{l}", [4 * 416, OWN], BF16) for l in range(L)]

    P = Prog(nc)
    P.bar_deps = []
    P.bar_need = set()

    orig_add = P.add

    def add(eng, fn, r=(), w=(), dma_sem=None, nobar=False, grp=False):
        if P.dry:
            return None
        op = orig_add(eng, fn, r, w, dma_sem, grp)
        if eng in P.bar_need and not nobar:
            P.bar_need.discard(eng)
            have = {d.idx for d in op.deps}
            for d in P.bar_deps:
                if d.idx not in have and d is not op:
                    if eng == "pe" and d.eng == "pe" and d.dma_sem is None:
                        continue
                    op.deps.append(d)
                    d.signal = True
        return op

    P.add = add
    P.last_dma = {}

    def barrier():
        if P.dry:
            return
        deps = []
        for e in COMPUTE + QUEUES:
            for op in reversed(P.ops[e]):
                if op.dma_sem is None:
                    deps.append(op)
                    break
        for e in COMPUTE + QUEUES:
            seen = set()
            for op in reversed(P.ops[e]):
                if op.dma_sem is not None and op.dma_sem not in seen:
                    seen.add(op.dma_sem)
                    deps.append(op)
        P.bar_deps = deps
        P.bar_need = set(COMPUTE + QUEUES)

    fx = Frame(nc, SBUF_BASE, SBUF_END, "fx")
    XS = fx.alloc("XS", [128, KC, TS], F32)
    H = fx.alloc("H", [128, KC, TS], BF16)
    ONES = fx.alloc("ONES", [128, 128], BF16)
    BD = fx.alloc("BD", [128, 128], BF16)
    SEL = fx.alloc("SEL", [128, 96], BF16)
    VECS = fx.alloc("VECS", [128, L, NV], F32)
    CV = fx.alloc("CV", [128, KC, 2], F32)
    CSIL = fx.alloc("CSIL", [128, KC, 2], F32)
    MOD = fx.alloc("MOD", [128, L, 2, 48], F32)
    SCL = fx.alloc("SCL", [128, L, 2, 16], F32)
    EPSV = fx.alloc("EPSV", [128, 1], F32)
    NSLOT = 5
    WS = [fx.alloc(f"W{i}", [128, 4096], BF16) for i in range(NSLOT)]
    PH0 = fx.cur
    PH_END = SBUF_END
    XP_OFF = PH_END - KC * TP * 4

    psum = []
    es = contextlib.ExitStack()
    for b in range(8):
        psum.append(es.enter_context(nc.psum_tensor(f"ps{b}", [128, 512], F32)))

    def wview(slot, shape):
        _UID[0] += 1
        n = int(np.prod(shape[1:]))
        return WS[slot][:, 0:n].rearrange(
            "p (a b c) -> p a b c" if len(shape) == 4 else ("p (a b) -> p a b" if len(shape) == 3 else "p a -> p a"),
            **({"a": shape[1], "b": shape[2]} if len(shape) == 4 else ({"a": shape[1]} if len(shape) == 3 else {})))

    class WStream:
        def __init__(self):
            self.reqs, self.pos, self.issued, self.depth = [], 0, 0, NSLOT - 2

        def get(self, loader):
            if P.dry:
                self.reqs.append(loader)
                return 0
            i = self.pos
            self.pos += 1
            while self.issued < min(len(self.reqs), i + self.depth + 1):
                k = self.issued
                self.reqs[k](k % NSLOT)
                self.issued += 1
            return i % NSLOT

    WSTR = WStream()

    def wload(slot, dst_ap, src_ap):
        P.add("pool", lambda e: e.dma_start(out=dst_ap, in_=src_ap), (), [("W", slot)],
              dma_sem=f"w{slot}", nobar=True, grp=True)

    def win_block(l, c0, nc_):
        def ld(slot):
            wload(slot, wview(slot, [128, KC, nc_]),
                  win_d[l].rearrange("(k p) c -> p k c", p=128)[:, :, c0:c0 + nc_])
        return ld

    def tcols(G, ti):
        off, n = G.tiles[ti]
        return off, n

    def segv(buf, G, ti, pad, shift=0):
        off, n = G.tiles[ti]
        if G.nseg == 1:
            return buf[:, 0, pad + shift + off: pad + shift + off + n]
        s0 = off // G.seglen
        ns = n // G.seglen
        return buf[:, s0:s0 + ns, pad + shift: pad + shift + G.seglen]

    def psv(ap, G, ti):
        if G.nseg == 1:
            return ap
        return ap.rearrange("p (s l) -> p s l", l=G.seglen)

    def proj(G, ti, lhs_of_kc, M, Hb=None):
        off, n = G.tiles[ti]
        b = P.bank()
        for kc in range(KC):
            MM(P, psum[b][0:M, 0:n], lhs_of_kc(kc), H[:, kc, off:off + n], kc == 0, kc == KC - 1,
               [("H", G.name, ti)] + lhs_of_kc.deps, [("B", b)])
        return b

    def mk_lhs(fn, deps):
        fn.deps = deps
        return fn

    def rms_rstd(G, sq_aps, lhsT, n, inv_cnt, out_rstd, rkeys, wkey, M=128):
        b = P.bank()
        for i, sq in enumerate(sq_aps):
            MM(P, psum[b][0:M, 0:n], lhsT, sq, i == 0, i == len(sq_aps) - 1, rkeys + [("CONST",)], [("B", b)])
        ACT(P, out_rstd, psum[b][0:M, 0:n], AF.Ln, [("B", b), ("CONST",)], [wkey], bias=EPSV[0:M, 0:1], scale=inv_cnt)
        ACT(P, out_rstd, out_rstd, AF.Exp, [wkey], [wkey], scale=-0.5)
        P.free(b)

    def norm_mod(G, X, fr, scale_ap, shift_ap, l):
        SQ = fr.alloc("SQ", [128, KC, 512], BF16)
        RSTD = fr.alloc("RSTD", [128, 512], F32)
        TM = [fr.alloc(f"TM{i}", [128, 512], F32) for i in range(2)]
        for ti, (off, n) in enumerate(G.tiles):
            for kc in range(KC):
                ACT(P, SQ[:, kc, 0:n], X[:, kc, off:off + n], AF.Square, [("X", G.name, ti)], [("SQ", kc)])
            rms_rstd(G, [SQ[:, kc, 0:n] for kc in range(KC)], ONES[:, :], n, 1.0 / D, RSTD[:, 0:n],
                     [("SQ", kc) for kc in range(KC)], ("RSTD",))
            for kc in range(KC):
                t = TM[kc % 2]
                STT(P, "dve", t[:, 0:n], X[:, kc, off:off + n], scale_ap[:, kc:kc + 1], RSTD[:, 0:n],
                    ALU.mult, ALU.mult, [("X", G.name, ti), ("RSTD",), ("MODV",)], [("TM", kc % 2)])
                if shift_ap is None:
                    CP(P, "pool", H[:, kc, off:off + n], t[:, 0:n], [("TM", kc % 2)], [("H", G.name, ti)])
                else:
                    ACT(P, H[:, kc, off:off + n], t[:, 0:n], AF.Identity, [("TM", kc % 2), ("MODV",)], [("H", G.name, ti)],
                        bias=shift_ap[:, kc:kc + 1])

    def emit_all():
        WSTR.pos = 0
        MEMSET(P, "dve", ONES[:, :], 1.0, [("CONST",)])
        MEMSET(P, "dve", BD[:, :], 0.0, [("CONST",)])
        MEMSET(P, "dve", BD[0:64, 0:64], 1.0, [("CONST",)])
        MEMSET(P, "dve", BD[64:128, 64:128], 1.0, [("CONST",)])
        MEMSET(P, "dve", EPSV[:, :], EPS, [("CONST",)])
        P.add("pool", lambda e: e.dma_start(out=SEL[0:32, :], in_=sel_d.get()), (), [("CONST",)], dma_sem="miscp")
        DMA(P, "sp", VECS[:, :, :], vecs_d.get(), "vecs", (), [("VECS",)])
        DMA(P, "sp", CV[:, :, :], cvec_d.get(), "cv", (), [("CV",)])
        DMA(P, "sp", XS[:, :, :], xs_d.get(), "xs", (), [("X", "s", ti) for ti in range(3)])
        ACT(P, CSIL[:, :, :], CV[:, :, :], AF.Silu, [("CV",)], [("CSIL",)])
        fa = Frame(nc, PH0, PH_END, "ada")
        WA = [fa.alloc(f"WA{i}", [128, KC, 512], F32) for i in range(3)]
        nblk = 0
        for l in range(L):
            b = P.bank()
            for cb in range(12):
                s = nblk % 3
                nblk += 1
                DMA(P, "sp", WA[s][:, :, :], wada_d[l].rearrange("(k p) c -> p k c", p=128)[:, :, cb * 512:(cb + 1) * 512],
                    f"wa{s}", (), [("WA", s)])
                for cc in range(4):
                    c48 = cb * 4 + cc
                    for kc in range(KC):
                        MM(P, psum[b][:, 2 * c48:2 * c48 + 2], WA[s][:, kc, cc * 128:(cc + 1) * 128], CSIL[:, kc, :],
                           kc == 0, kc == KC - 1, [("WA", s), ("CSIL",)], [("B", b)])
            for j in range(2):
                TT(P, "dve", MOD[:, l, j, :], psum[b][:, 0:96].rearrange("p (c j) -> p c j", j=2)[:, :, j],
                   VECS[:, l, V_BADA:V_BADA + 48], ALU.add, [("B", b), ("VECS",)], [("MODV",)])
                STT(P, "dve", SCL[:, l, j, 0:8], MOD[:, l, j, 8:16], 1.0, VECS[:, l, V_GMIX:V_GMIX + 8],
                    ALU.add, ALU.mult, [("MODV",), ("VECS",)], [("MODV",)])
                STT(P, "dve", SCL[:, l, j, 8:16], MOD[:, l, j, 32:40], 1.0, VECS[:, l, V_GFFN:V_GFFN + 8],
                    ALU.add, ALU.mult, [("MODV",), ("VECS",)], [("MODV",)])
            P.free(b)
        barrier()
        if 'r' in SKIP and not P.dry:
            P.free_banks = [b_ for b_ in P.free_banks if b_ not in (0, 1)]
        P.stage = 0
        for l in range(L):
            for G_ in ((GP,) if os.environ.get('K_NOSAMPLE') else (GP, GS)):
                layer(G_, l)
                if P.stage >= STOP:
                    return
                if l == L - 1 and G_ is GP:
                    final_norm(GP)
        if not os.environ.get('K_NOSAMPLE'):
            final_norm(GS)

    def layer(G, l):
        gn = G.name
        j = 1 if G.sample else 0
        T = G.T
        ntile = len(G.tiles)
        lim = PH_END if G.sample else XP_OFF
        if G.sample:
            X = XS
        else:
            _UID[0] += 1
            X = nc.alloc_sbuf_tensor_at(f"XP_{l}_{_UID[0]}", [128, KC, TP], F32, offset=XP_OFF)
            src = xp_d.get() if l == 0 else xp_scr
            DMA(P, "sp", X[:, :, :], src, "xp", [("XPSCR",)], [("X", "p", ti) for ti in range(ntile)])
        vec = lambda c0, n=1: VECS[:, l, c0:c0 + n]
        fr = Frame(nc, PH0, lim, f"L{l}{gn}")
        QAN = fr.alloc("QAN", [128, 2, T], BF16)
        QG = fr.alloc("QG", [128, 4, T], BF16)
        AOUT = fr.alloc("AOUT", [128, 4, T], BF16)
        BOUT = fr.alloc("BOUT", [128, 4, T], BF16)
        COUT = fr.alloc("COUT", [128, 4, T], BF16)
        S0 = fr.cur
        ROPE32 = None
        if G.sample:
            _UID[0] += 1
            ROPE32 = nc.alloc_sbuf_tensor_at(f"ROPE32_{l}_{_UID[0]}", [128, 2, TS], F32, offset=S0 - 4 * T * 2)
            DMA(P, "sp", ROPE32[:, :, :], rope32_d.get(), ("tab", 3), (), [("ROPE32",)])

        f1 = Frame(nc, S0, lim, f"L{l}{gn}p1")
        norm_mod(G, X, f1, SCL[:, l, j, 0:8], MOD[:, l, j, 0:8], l)
        barrier()
        P.stage += 1
        if P.stage >= STOP:
            return
        f1 = Frame(nc, S0, lim, f"L{l}{gn}p1b")
        if not G.sample:
            CKVT = f1.alloc("CKVT", [128, T], BF16)
            KROPET = f1.alloc("KROPET", [128, T], BF16)
            KDUP = f1.alloc("KDUP", [128, 2, T], BF16)
            VAUG = f1.alloc("VAUG", [128, T // 128, 2, 192], BF16)
            if 'm' not in SKIP:
                MEMSET(P, "pool", VAUG[:, :, :, 64:128], 1.0, [("VAUG", "a")])
        kv_end = f1.cur
        ZQ = f1.alloc("ZQ", [128, 2, 512], F32)
        SQ2 = f1.alloc("SQ2", [128, 2, 512], BF16)
        T1 = f1.alloc("T1", [128, 512], F32)
        T2 = f1.alloc("T2", [128, 512], F32)
        RS2 = f1.alloc("RS2", [128, 512], F32)
        KVF = [f1.alloc(f"KVF{i}", [128, 512], F32) for i in range(2)]
        SQ3 = f1.alloc("SQ3", [128, 512], BF16)
        RS3 = f1.alloc("RS3", [128, 512], F32)
        WKD = f1.alloc("WKD", [128, KC, 2, 128], BF16)
        if G.sample:
            ROPE64 = f1.alloc("ROPE64", [128, 2, TS], F32)
            DMA(P, "sp", ROPE64[:, :, :], rope64_d.get(), ("tab", 3), (), [("ROPE64",)])
            WPM = f1.alloc("WPM", [128, KC, 512], BF16)
            WKDP = f1.alloc("WKDP", [128, KC, 2, 128], BF16)
            SCKV = f1.alloc("SCKV", [128, OWN], BF16)
            SKR = f1.alloc("SKR", [128, OWN], BF16)
            SKG = f1.alloc("SKG", [128, 2, OWN], BF16)
            SV = f1.alloc("SV", [128, 8, 128], BF16)
        else:
            pass
        def own(ti):
            off, n = G.tiles[ti]
            lo, hi = max(off, HALO), min(off + n, HALO + OWN)
            return lo - off, hi - off, lo - HALO

        sA = WSTR.get(win_block(l, 0, 416))
        WA_ = wview(sA, [128, KC, 416])
        if 'P' in SKIP:
            P.stage = 10 ** 6
            return
        if G.sample:
            for half in range(2):
                CP(P, "pool", WPM[:, :, 0:32].rearrange("p k (a h e) -> p k a h e", a=2, h=2)[:, :, :, half, :],
                   WA_[:, :, C_KR:C_KR + 32].rearrange("p k (a h e) -> p k a h e", a=2, h=2)[:, :, :, 1 - half, :],
                   [("W", sA)], [("WPM",)])
        for ti, (off, n) in enumerate(G.tiles):
            for c in range(0 if 'q' in SKIP else 2):
                b = proj(G, ti, mk_lhs(lambda kc, c=c: WA_[:, kc, c * 128:(c + 1) * 128], [("W", sA)]), 128)
                if 'Q' in SKIP:
                    CP(P, "dve", ZQ[:, c, 0:n], psum[b][:, 0:n], [("B", b)], [("ZQ", c)])
                    P.free(b)
                    continue
                ACT(P, SQ2[:, c, 0:n], psum[b][:, 0:n], AF.Square, [("B", b)], [("SQ2", c)])
                CP(P, "dve", ZQ[:, c, 0:n], psum[b][:, 0:n], [("B", b)] + ([("SQ2", c)] if 'T' in SKIP else []), [("ZQ", c)])
                P.free(b)
            if 'Q' in SKIP:
                continue
            if 'S' in SKIP:
                continue
            if 'q' not in SKIP:
                rms_rstd(G, [SQ2[:, c, 0:n] for c in range(2)], ONES[:, :], n, 1.0 / 256, RS2[:, 0:n],
                         [("SQ2", 0), ("SQ2", 1)], ("RS2",))
            if 'R' in SKIP:
                continue
            for c in range(0 if 'q' in SKIP else 2):
                STT(P, "dve", QAN[:, c, off:off + n], ZQ[:, c, 0:n], vec(V_GQA + c), RS2[:, 0:n], ALU.mult, ALU.mult,
                    [("ZQ", c), ("RS2",), ("VECS",)], [("QAN", ti)])
            if 'c' in SKIP:
                continue
            if 'b' in SKIP:
                barrier()
            b = proj(G, ti, mk_lhs(lambda kc: WA_[:, kc, C_KV:C_KV + 128], [("W", sA)]), 128)
            if 'v' in SKIP:
                ACT(P, SQ3[:, 0:n], psum[b][:, 0:n], AF.Square, [("B", b)], [("SQ3",)])
                rms_rstd(G, [SQ3[:, 0:n]], ONES[:, :], n, 1.0 / 128, RS3[:, 0:n], [("SQ3",)], ("RS3",))
                STT(P, "dve", KVF[0][:, 0:n], psum[b][:, 0:n], vec(V_GKVA), RS3[:, 0:n], ALU.mult, ALU.mult,
                    [("B", b), ("RS3",), ("VECS",)], [("KVF", 0)])
                P.free(b)
                continue
            ACT(P, SQ3[:, 0:n], psum[b][:, 0:n], AF.Square, [("B", b)], [("SQ3",)])
            rms_rstd(G, [SQ3[:, 0:n]], ONES[:, :], n, 1.0 / 128, RS3[:, 0:n], [("SQ3",)], ("RS3",))
            RS2_ = RS3
            if G.sample:
                lo, hi, so = own(ti)
                STT(P, "dve", SCKV[:, so:so + hi - lo], psum[b][:, lo:hi], vec(V_GKVA), RS3[:, lo:hi], ALU.mult, ALU.mult,
                    [("B", b), ("RS3",), ("VECS",)], [("SCKV",)])
            else:
                kf = KVF[0]
                if 'x' in SKIP:
                    CP(P, "dve", ZQ[:, 0, 0:n], psum[b][:, 0:n], [("B", b)], [("ZQ", 0)])
                    STT(P, "dve", kf[:, 0:n], ZQ[:, 0, 0:n], vec(V_GKVA), RS2[:, 0:n], ALU.mult, ALU.mult,
                        [("ZQ", 0), ("RS2",), ("VECS",)], [("KVF", 0)])
                else:
                    STT(P, "dve", kf[:, 0:n], psum[b][:, 0:n], vec(V_GKVA), RS3[:, 0:n], ALU.mult, ALU.mult,
                        [("B", b), ("RS3",), ("VECS",)], [("KVF", 0)])
                DMA(P, "sp", ockv_d[l][:, off:off + n], kf[:, 0:n], ("out", 6), [("KVF", 0)], [])
                if 'y' in SKIP:
                    CP(P, "pool", CKVT[:, off:off + n], kf[:, 0:n], [("KVF", 0)], [("CKVT", "a")])
                elif 'z' not in SKIP:
                    CP(P, "act", CKVT[:, off:off + n], kf[:, 0:n], [("KVF", 0)], [("CKVT", "a")])
            P.free(b)
            if 'k' in SKIP:
                continue
            b = proj(G, ti, mk_lhs(lambda kc: WA_[:, kc, C_KR:C_KR + 32], [("W", sA)]), 32)
            if G.sample:
                b2 = proj(G, ti, mk_lhs(lambda kc: WPM[:, kc, 0:32], [("WPM",)]), 32)
                lo, hi, so = own(ti)
                w_ = hi - lo
                TT(P, "dve", T1[0:32, 0:w_], psum[b][0:32, lo:hi], ROPE32[0:32, 0, off + lo:off + hi], ALU.mult,
                   [("B", b), ("ROPE32",)], [("T1",)])
                TT(P, "dve", T2[0:32, 0:w_], psum[b2][0:32, lo:hi], ROPE32[0:32, 1, off + lo:off + hi], ALU.mult,
                   [("B", b2), ("ROPE32",)], [("T2",)])
                TT(P, "pool", SKR[0:32, so:so + w_], T1[0:32, 0:w_], T2[0:32, 0:w_], ALU.add, [("T1",), ("T2",)], [("SKR",)])
                P.free(b2)
            else:
                kf = KVF[1]
                CP(P, "act", kf[0:32, 0:n], psum[b][0:32, 0:n], [("B", b)], [("KVF", 1)])
                DMA(P, "sp", okr_d[l][:, off:off + n], kf[0:32, 0:n], ("out", 6), [("KVF", 1)], [])
                CP(P, "pool", KROPET[0:32, off:off + n], kf[0:32, 0:n], [("KVF", 1)], [("KROPET", "a")])
            P.free(b)

        if SUB <= 1:
            P.stage = 10 ** 6
            return
        sC = WSTR.get(win_block(l, C_GK, 256))
        WC_ = wview(sC, [128, KC, 256])
        for g in range(2):
            for dup in range(2):
                CP(P, "pool", WKD[:, :, g, dup * 64:(dup + 1) * 64], WC_[:, :, g * 64:(g + 1) * 64], [("W", sC)], [("WKD",)])
        if G.sample:
            for half in range(2):
                CP(P, "pool", WKDP.rearrange("p k g (a h e) -> p k (g a) h e", a=4, h=2)[:, :, :, half, :],
                   WKD.rearrange("p k g (a h e) -> p k (g a) h e", a=4, h=2)[:, :, :, 1 - half, :], [("WKD",)], [("WKDP",)])

        if SUB <= 2:
            P.stage = 10 ** 6
            return

        def headnorm_rope(b, b2, n, gv, gpv, out_ap, lo, hi, off, rk, wk):
            ACT(P, SQ2[:, 0, 0:n], psum[b][:, 0:n], AF.Square, [("B", b)], [("SQ2", 0)])
            rms_rstd(G, [SQ2[:, 0, 0:n]], BD[:, :], n, 1.0 / 64, RS2[:, 0:n], [("SQ2", 0)], ("RS2",))
            w_ = hi - lo
            if b2 is None:
                STT(P, "dve", out_ap, psum[b][:, lo:hi], gv, RS2[:, lo:hi], ALU.mult, ALU.mult,
                    [("B", b), ("RS2",), ("VECS",)] + rk, wk)
            else:
                STT(P, "dve", T1[:, 0:w_], psum[b][:, lo:hi], gv, ROPE64[:, 0, off + lo:off + hi], ALU.mult, ALU.mult,
                    [("B", b), ("ROPE64",), ("VECS",)], [("T1",)])
                STT(P, "dve", T2[:, 0:w_], psum[b2][:, lo:hi], gpv, ROPE64[:, 1, off + lo:off + hi], ALU.mult, ALU.mult,
                    [("B", b2), ("ROPE64",), ("VECS",)], [("T2",)])
                TT(P, "pool", T1[:, 0:w_], T1[:, 0:w_], T2[:, 0:w_], ALU.add, [("T1",), ("T2",)], [("T1",)])
                TT(P, "dve", out_ap, T1[:, 0:w_], RS2[:, lo:hi], ALU.mult, [("T1",), ("RS2",)] + rk, wk)

        for ti, (off, n) in enumerate(G.tiles):
            for g in range(2):
                b = proj(G, ti, mk_lhs(lambda kc, g=g: WKD[:, kc, g, :], [("WKD",)]), 128)
                if G.sample:
                    b2 = proj(G, ti, mk_lhs(lambda kc, g=g: WKDP[:, kc, g, :], [("WKDP",)]), 128)
                    lo, hi, so = own(ti)
                    headnorm_rope(b, b2, n, vec(V_GK), vec(V_GKP), SKG[:, g, so:so + hi - lo], lo, hi, off, [], [("SKG",)])
                    P.free(b2)
                else:
                    kf = KVF[g]
                    headnorm_rope(b, None, n, vec(V_GK), None, kf[:, 0:n], 0, n, off, [], [("KVF", g)])
                    DMA(P, "sp", ok_d[l][g][:, off:off + n], kf[0:64, 0:n], ("out", 6), [("KVF", g)], [])
                    CP(P, "act", KDUP[:, g, off:off + n], kf[:, 0:n], [("KVF", g)], [("KDUP", "a")])
                P.free(b)
        if SUB <= 3:
            P.stage = 10 ** 6
            return
        nblk = 8
        for i in range(nblk):
            c0 = (HALO if G.sample else 0) + i * 128
            tis = sorted({ti for ti, (off, n) in enumerate(G.tiles) if off < c0 + 128 and off + n > c0})
            b = P.bank()
            for kc in range(KC):
                MM(P, psum[b][:, 0:128], H[:, kc, c0:c0 + 128], WC_[:, kc, 128:256], kc == 0, kc == KC - 1,
                   [("H", gn, ti) for ti in tis] + [("W", sC)], [("B", b)])
            if G.sample:
                CP(P, "act", SV[:, i, :], psum[b][:, 0:128], [("B", b)], [("SV",)])
            else:
                kf = KVF[i % 2]
                CP(P, "act", kf[:, 0:128], psum[b][:, 0:128], [("B", b)], [("KVF", i % 2)])
                DMA(P, "sp", ov_d[l][:, i, :], kf[:, 0:128], ("out", 6), [("KVF", i % 2)], [])
                for o2 in (0, 128):
                    CP(P, "dve", VAUG[:, i, :, o2:o2 + 64], psum[b][:, 0:128].rearrange("p (g d) -> p g d", g=2),
                       [("B", b)], [("VAUG", "a")])
            P.free(b)

        if G.sample:
            snd, rcv = send_t[l].ap(), recv_t[l].ap()
            DMA(P, "sp", snd[0:128, :], SCKV[:, :], "snd", [("SCKV",)], [("SEND",)], grp=True)
            for g in range(2):
                DMA(P, "sp", snd[128 + 64 * g:192 + 64 * g, :], SKG[0:64, g, :], "snd", [("SKG",)], [("SEND",)], grp=True)
            DMA(P, "sp", snd[256:288, :], SKR[0:32, :], "snd", [("SKR",)], [("SEND",)], grp=True)
            DMA(P, "sp", snd[288:416, :].rearrange("p (b c) -> p b c", c=128), SV[:, :, :], "snd", [("SV",)], [("SEND",)], grp=True)
            P.add("pool", lambda e: e.collective_compute("AllGather", ALU.bypass,
                                                         replica_groups=[[0, 1, 2, 3], [4, 5, 6, 7]],
                                                         ins=[snd], outs=[rcv]),
                  [("SEND",)], [("RECV",)], dma_sem=f"cc{l}")

        if SUB <= 4:
            P.stage = 10 ** 6
            return
        sB = WSTR.get(win_block(l, C_GQ, 512))
        WB_ = wview(sB, [128, KC, 512])
        if G.sample:
            for half in range(2):
                CP(P, "pool", WPM.rearrange("p k (a h e) -> p k a h e", a=16, h=2)[:, :, :, half, :],
                   WB_.rearrange("p k (a h e) -> p k a h e", a=16, h=2)[:, :, :, 1 - half, :], [("W", sB)], [("WPM",)])
        for ti, (off, n) in enumerate(G.tiles):
            for c in range(4):
                b = proj(G, ti, mk_lhs(lambda kc, c=c: WB_[:, kc, c * 128:(c + 1) * 128], [("W", sB)]), 128)
                b2 = None
                if G.sample:
                    b2 = proj(G, ti, mk_lhs(lambda kc, c=c: WPM[:, kc, c * 128:(c + 1) * 128], [("WPM",)]), 128)
                headnorm_rope(b, b2, n, vec(V_GQ), vec(V_GQP), QG[:, c, off:off + n], 0, n, off, [], [("QG", ti)])
                if b2 is not None:
                    P.free(b2)
                P.free(b)
        barrier()
        P.stage += 1
        if P.stage >= STOP:
            return

        f2 = Frame(nc, (S0 if G.sample else kv_end), lim, f"L{l}{gn}p2")
        if G.sample:
            SK = S_ALL
            CKVT = f2.alloc("CKVT", [128, SK], BF16)
            KROPET = f2.alloc("KROPET", [128, SK], BF16)
            rv = recv_t[l].ap().rearrange("(r w) c -> w r c", w=416)
            DMA(P, "sp", CKVT[:, 0:4096].rearrange("p (r c) -> p r c", r=4), rv[0:128], "k1", [("RECV",)], [("CKVT", "a")])
            DMA(P, "sp", KROPET[0:32, 0:4096].rearrange("p (r c) -> p r c", r=4), rv[256:288], "k2", [("RECV",)], [("KROPET", "a")])
            P.add("pool", lambda e: e.dma_start(out=CKVT[:, 4096:SK], in_=cckv_d[l]), (), [("CKVT", "c")], dma_sem="k1c")
            P.add("pool", lambda e: e.dma_start(out=KROPET[0:32, 4096:SK], in_=ckr_d[l]), (), [("KROPET", "c")], dma_sem="k2c")
        else:
            SK = T
        st0 = f2.cur
        nkb = SK // 128
        if G.sample:
            KDUP = f2.alloc("KDUP", [128, 2, SK], BF16)
            VAUG = f2.alloc("VAUG", [128, nkb, 2, 192], BF16)
            for g in range(2):
                for hf in range(2):
                    DMA(P, "sp", KDUP[64 * hf:64 * hf + 64, g, 0:4096].rearrange("p (r c) -> p r c", r=4),
                        rv[128 + 64 * g:192 + 64 * g], "k3", [("RECV",)], [("KDUP", "a")], grp=True)
                    P.add("pool", lambda e, g=g, hf=hf: e.dma_start(out=KDUP[64 * hf:64 * hf + 64, g, 4096:SK], in_=ck_d[l][g]),
                          (), [("KDUP", "c")], dma_sem="k3c", grp=True)
            MEMSET(P, "pool", VAUG[:, :, :, 64:128], 1.0, [("VAUG", "a")])
            for o2 in (0, 128):
                for r_ in range(4):
                    DMA(P, "sp", VAUG.rearrange("p b g c -> p (b g) c")[:, r_ * 16:(r_ + 1) * 16, o2:o2 + 64],
                        rv[288:416, r_, :].rearrange("p (bg d) -> p bg d", d=64), "k4", [("RECV",)], [("VAUG", "a")], grp=True)
                P.add("pool", lambda e, o2=o2: e.dma_start(out=VAUG.rearrange("p b g c -> p (b g) c")[:, 64:72, o2:o2 + 64],
                                                           in_=cv_d[l].rearrange("p b (g d) -> p (b g) d", g=2)),
                      (), [("VAUG", "c")], dma_sem="k4c", grp=True)
        fpt = Frame(nc, (st0 + 46080 if G.sample else f2.cur), lim, f"L{l}{gn}pt")
        PT = [fpt.alloc(f"PT{i}", [128, 512], BF16) for i in range(4)]
        RCP = fpt.alloc("RCP", [128, 512], F32)
        pt_i = [0]

        if G.sample:
            qranges = [(off, n, 0, nkb) for (off, n) in G.tiles]
        else:
            qranges = [(s * 256, 256, 2 * s, 2) for s in range(4)]

        def attend(q_of, k_of, v_of, Kp, p0, scale, out_ap_of, rq, rk, rv_, wout):
            for (off, n, kb0, nk) in qranges:
                bo = P.bank()
                pend = []

                def qk(kb):
                    bs = P.bank()
                    MM(P, psum[bs][:, 0:n], k_of(kb), q_of(off, n), True, True, rq + rk, [("B", bs)])
                    return bs

                def pv(kb, bs, first, last):
                    pi = pt_i[0] % 4
                    pt_i[0] += 1
                    ACT(P, PT[pi][:, 0:n], psum[bs][:, 0:n], AF.Exp, [("B", bs)], [("PT", pi)], scale=scale)
                    P.free(bs)
                    MM(P, psum[bo][:, 0:n], v_of(kb), PT[pi][:, 0:n], first, last, [("PT", pi)] + rv_, [("B", bo)])

                kbs = list(range(kb0, kb0 + nk))
                DEPTH = 6
                for idx in range(len(kbs)):
                    pend.append((kbs[idx], qk(kbs[idx])))
                    if len(pend) > DEPTH:
                        kb, bs = pend.pop(0)
                        pv(kb, bs, kb == kbs[0], False)
                while pend:
                    kb, bs = pend.pop(0)
                    pv(kb, bs, kb == kbs[0], len(pend) == 0)
                s0_ = 64 - p0
                RECIP(P, RCP[p0:p0 + 64, 0:n], psum[bo][s0_:s0_ + 64, 0:n], [("B", bo)], [("RCP", p0)])
                TT(P, "dve", out_ap_of(off, n), psum[bo][p0:p0 + 64, 0:n], RCP[p0:p0 + 64, 0:n], ALU.mult,
                   [("B", bo), ("RCP", p0)], wout)
                P.free(bo)

        for hd in range(8):
            g, p0, c = hd // 4, 64 * (hd % 2), hd // 2
            attend(lambda off, n: QG[p0:p0 + 64, c, off:off + n],
                   lambda kb: KDUP[p0:p0 + 64, g, kb * 128:(kb + 1) * 128],
                   lambda kb: VAUG[:, kb, g, (64 if p0 else 0):(64 if p0 else 0) + 128],
                   64, p0, 64 ** -0.5,
                   lambda off, n: BOUT[p0:p0 + 64, c, off:off + n],
                   [("QG", ti) for ti in range(ntile)], [("KDUP", "a"), ("KDUP", "c")], [("VAUG", "a"), ("VAUG", "c")], [("BOUT",)])
        barrier()
        P.stage += 1
        if P.stage >= STOP:
            return

        f3 = Frame(nc, st0 if G.sample else fpt.cur, lim, f"L{l}{gn}p3")
        KH = [f3.alloc(f"KH{i}", [128, SK], BF16) for i in range(2)]
        VH = [f3.alloc(f"VH{i}", [128, nkb, 128], BF16) for i in range(2)]
        QH = [f3.alloc(f"QH{i}", [128, T], BF16) for i in range(2)]
        if G.sample:
            assert f3.cur <= st0 + 46080
        MEMSET(P, "pool", VH[0][:, :, 64:128], 1.0, [("VH", 0)])
        MEMSET(P, "pool", VH[1][:, :, 0:64], 1.0, [("VH", 1)])

        def ld_small(slot):
            wload(slot, wview(slot, [128, 2, 768]), wqb_d[l].rearrange("(k p) c -> p k c", p=128))
            P.add("pool", lambda e: e.dma_start(out=WS[slot][:, 1536:2560], in_=wkvb_d[l]), (), [("W", slot)],
                  dma_sem=f"w{slot}", nobar=True, grp=True)
        sM = WSTR.get(ld_small)
        WQB = wview(sM, [128, 2, 768])
        WKVB = WS[sM][:, 1536:2560].rearrange("p (h c) -> p h c", h=8)
        WKN = WS[sM][:, 2560:2560 + 768].rearrange("p (h c) -> p h c", h=8)
        WQP = f3.alloc("WQP", [128, 2, 8, 96], BF16) if G.sample else None
        MEMSET(P, "pool", WKN[:, :, 64:96], 0.0, [("WKN",)])
        CP(P, "pool", WKN[:, :, 0:64], WKVB[:, :, 0:64], [("W", sM)], [("WKN",)])
        if G.sample:
            MEMSET(P, "pool", WQP[:, :, :, 0:64], 0.0, [("WQP",)])
            for k2 in range(2):
                for half in range(2):
                    CP(P, "pool", WQP[:, k2, :, 64:96].rearrange("p h (a f e) -> p h a f e", a=2, f=2)[:, :, :, half, :],
                       WQB[:, k2, :].rearrange("p (h c) -> p h c", h=8)[:, :, 64:96].rearrange(
                           "p h (a f e) -> p h a f e", a=2, f=2)[:, :, :, 1 - half, :], [("W", sM)], [("WQP",)])
        for h in range(8):
            i2 = h % 2
            p0 = 64 * i2
            for c0 in range(0, SK, 512):
                b = P.bank()
                MM(P, psum[b][0:96, 0:512], WKN[:, h, :], CKVT[:, c0:c0 + 512], True, False, [("WKN",), ("CKVT", "a"), ("CKVT", "c")], [("B", b)])
                MM(P, psum[b][0:96, 0:512], SEL[0:32, :], KROPET[0:32, c0:c0 + 512], False, True,
                   [("CONST",), ("KROPET", "a"), ("KROPET", "c")], [("B", b)])
                CP(P, "dve", KH[i2][0:96, c0:c0 + 512], psum[b][0:96, 0:512], [("B", b)], [("KH", i2)])
                P.free(b)
            for k0 in range(0, nkb, 8):
                nb_ = min(8, nkb - k0)
                b = P.bank()
                for i in range(nb_):
                    MM(P, psum[b][:, i * 64:(i + 1) * 64], CKVT[:, (k0 + i) * 128:(k0 + i + 1) * 128], WKVB[:, h, 64:128],
                       True, True, [("CKVT", "a"), ("CKVT", "c"), ("W", sM)], [("B", b)])
                CP(P, "act", VH[i2][:, k0:k0 + nb_, p0:p0 + 64], psum[b][:, 0:nb_ * 64].rearrange("p (b d) -> p b d", d=64),
                   [("B", b)], [("VH", i2)])
                P.free(b)
            for ti, (off, n) in enumerate(G.tiles):
                b = P.bank()
                for k2 in range(2):
                    MM(P, psum[b][0:96, 0:n], WQB[:, k2, h * 96:(h + 1) * 96], QAN[:, k2, off:off + n], k2 == 0, k2 == 1,
                       [("W", sM), ("QAN", ti)], [("B", b)])
                if G.sample:
                    b2 = P.bank()
                    for k2 in range(2):
                        MM(P, psum[b2][0:96, 0:n], WQP[:, k2, h, :], QAN[:, k2, off:off + n], k2 == 0, k2 == 1,
                           [("WQP",), ("QAN", ti)], [("B", b2)])
                    CP(P, "act", QH[i2][0:64, off:off + n], psum[b][0:64, 0:n], [("B", b)], [("QH", i2)])
                    TT(P, "dve", RCP[64:96, 0:n], psum[b][64:96, 0:n], ROPE32[64:96, 0, off:off + n], ALU.mult,
                       [("B", b), ("ROPE32",)], [("RCP", 64)])
                    TT(P, "dve", PT[0][64:96, 0:n].bitcast(BF16), psum[b2][64:96, 0:n], ROPE32[64:96, 1, off:off + n], ALU.mult,
                       [("B", b2), ("ROPE32",)], [("PT", 0)]) if False else None
                    TT(P, "dve", QH[i2][64:96, off:off + n], psum[b2][64:96, 0:n], ROPE32[64:96, 1, off:off + n], ALU.mult,
                       [("B", b2), ("ROPE32",)], [("QH", i2)])
                    TT(P, "dve", QH[i2][64:96, off:off + n], QH[i2][64:96, off:off + n], RCP[64:96, 0:n], ALU.add,
                       [("QH", i2), ("RCP", 64)], [("QH", i2)])
                    P.free(b2)
                else:
                    CP(P, "act", QH[i2][0:96, off:off + n], psum[b][0:96, 0:n], [("B", b)], [("QH", i2)])
                P.free(b)
            attend(lambda off, n: QH[i2][0:96, off:off + n],
                   lambda kb: KH[i2][0:96, kb * 128:(kb + 1) * 128],
                   lambda kb: VH[i2][:, kb, :],
                   96, p0, 96 ** -0.5,
                   lambda off, n: AOUT[p0:p0 + 64, h // 2, off:off + n],
                   [("QH", i2)], [("KH", i2)], [("VH", i2)], [("AOUT",)])
        barrier()
        P.stage += 1
        if P.stage >= STOP:
            return

        f4 = Frame(nc, S0, lim, f"L{l}{gn}p4")
        Lp = G.seglen + 16
        PIN = f4.alloc("PIN", [128, G.nseg, Lp], F32)
        PA = f4.alloc("PA", [128, G.nseg, Lp], F32)
        PB = f4.alloc("PB", [128, G.nseg, Lp], F32)
        PTAB = f4.alloc("PTAB", [128, T], F32)
        POOLED = f4.alloc("POOLED", [128, T], BF16)
        if G.sample:
            MASK = f4.alloc("MASK", [128, TS], F32)
            DMA(P, "sp", MASK[:, :], mask_d.get(), ("tab", 3), (), [("MASK",)])
        MEMSET(P, "pool", PIN[:, :, :], 0.0, [("PIN",)])
        sD = WSTR.get(win_block(l, C_PO, 512))
        WD_ = wview(sD, [128, KC, 512])

        def ld_pool(slot):
            wload(slot, wview(slot, [128, 4, 128]), wpool_d[l].rearrange("g c d -> c g d"))
        sPW = WSTR.get(ld_pool)
        WPL = wview(sPW, [128, 4, 128])
        SL = G.seglen
        for gi in range(4):
            DMA(P, "sp", PTAB[:, :], ptab_d[gn][:, gi, :], ("tab", 3), (), [("PTAB",)])
            for ti, (off, n) in enumerate(G.tiles):
                b = proj(G, ti, mk_lhs(lambda kc, gi=gi: WD_[:, kc, gi * 128:(gi + 1) * 128], [("W", sD)]), 128)
                if G.sample:
                    TT(P, "dve", segv(PIN, G, ti, 8), psum[b][:, 0:n], MASK[:, off:off + n], ALU.mult,
                       [("B", b), ("MASK",)], [("PIN",)])
                else:
                    CP(P, "act", segv(PIN, G, ti, 8), psv(psum[b][:, 0:n], G, ti), [("B", b)], [("PIN",)])
                P.free(b)
            shifts = [(-1, 0), (-1, 1), (-2, 2), (-4, 4)][:gi + 1]
            rng = [None] * (gi + 1)
            lo_, hi_ = 8, SL + 8
            for k in range(gi, -1, -1):
                rng[k] = (lo_, hi_)
                lo_, hi_ = lo_ + shifts[k][0], hi_ + shifts[k][1]
            src, srck = PIN, ("PIN",)
            bufs = [(PA, ("PA",)), (PB, ("PB",))]
            for k in range(gi + 1):
                dst, dstk = bufs[k % 2]
                a0, a1 = rng[k]
                s_lo, s_hi = shifts[k]
                TT(P, "pool", dst[:, :, a0:a1], src[:, :, a0 + s_lo:a1 + s_lo], src[:, :, a0 + s_hi:a1 + s_hi], ALU.add,
                   [srck], [dstk])
                src, srck = dst, dstk
            oth, othk = bufs[(gi + 1) % 2]
            TT(P, "pool", oth[:, :, 8:SL + 8], src[:, :, 8:SL + 8], PTAB[:, :].rearrange("p (s l) -> p s l", l=SL), ALU.mult,
               [srck, ("PTAB",)], [othk])
            TT(P, "pool", POOLED[:, :].rearrange("p (s l) -> p s l", l=SL), oth[:, :, 8:SL + 8], PIN[:, :, 8:SL + 8], ALU.subtract,
               [othk, ("PIN",)], [("POOLED",)])
            for ti, (off, n) in enumerate(G.tiles):
                b = P.bank()
                MM(P, psum[b][:, 0:n], WPL[:, gi, :], POOLED[:, off:off + n], True, True, [("W", sPW), ("POOLED",)], [("B", b)])
                ACT(P, COUT[:, gi, off:off + n], psum[b][:, 0:n], AF.Copy, [("B", b), ("VECS",), ("ROPE32",)], [("COUT",)],
                    scale=vec(V_PSC + gi))
                P.free(b)
        barrier()
        P.stage += 1
        if P.stage >= STOP:
            return

        f5 = Frame(nc, S0, lim, f"L{l}{gn}p5")
        MERGED = f5.alloc("MERGED", [128, KC, T], BF16)
        SG = [f5.alloc(f"SG{i}", [128, 512], F32) for i in range(3)]
        MT = [f5.alloc(f"MT{i}", [128, 512], F32) for i in range(3)]
        BR = [AOUT, BOUT, COUT]
        BRK = [("AOUT",), ("BOUT",), ("COUT",)]
        for jc in range(8):
            def ld_gate(slot, jc=jc):
                for b3 in range(3):
                    c0 = C_GT + b3 * 1024 + jc * 128
                    wload(slot, wview(slot, [128, KC, 3, 128])[:, :, b3, :],
                          win_d[l].rearrange("(k p) c -> p k c", p=128)[:, :, c0:c0 + 128])

            def ld_br(slot, jc=jc):
                for bi in range(3):
                    P.add("pool", lambda e, bi=bi: e.dma_start(
                        out=WS[slot][:, bi * 512:(bi + 1) * 512].rearrange("p (k c) -> p k c", k=4),
                        in_=wbr_d[bi][l].rearrange("(k p) c -> p k c", p=128)[:, :, jc * 128:(jc + 1) * 128]),
                        (), [("W", slot)], dma_sem=f"w{slot}", nobar=True, grp=True)
            sG = WSTR.get(ld_gate)
            sR = WSTR.get(ld_br)
            GW = wview(sG, [128, KC, 3, 128])
            BW = WS[sR][:, 0:1536].rearrange("p (b k c) -> p b k c", b=3, k=4)
            for ti, (off, n) in enumerate(G.tiles):
                for bi in range(3):
                    bg = proj(G, ti, mk_lhs(lambda kc, bi=bi: GW[:, kc, bi, :], [("W", sG)]), 128)
                    ACT(P, SG[bi][:, 0:n], psum[bg][:, 0:n], AF.Sigmoid, [("B", bg)], [("SG", bi)])
                    P.free(bg)
                    bb = P.bank()
                    for k4 in range(4):
                        MM(P, psum[bb][:, 0:n], BW[:, bi, k4, :], BR[bi][:, k4, off:off + n], k4 == 0, k4 == 3,
                           [("W", sR), BRK[bi]], [("B", bb)])
                    TT(P, "dve", MT[bi][:, 0:n], psum[bb][:, 0:n], SG[bi][:, 0:n], ALU.mult, [("B", bb), ("SG", bi)], [("MT", bi)])
                    P.free(bb)
                TT(P, "dve", MT[0][:, 0:n], MT[0][:, 0:n], MT[1][:, 0:n], ALU.add, [("MT", 0), ("MT", 1)], [("MT", 0)])
                TT(P, "dve", MERGED[:, jc, off:off + n], MT[0][:, 0:n], MT[2][:, 0:n], ALU.add, [("MT", 0), ("MT", 2)],
                   [("MERGED", ti)])
        for jb in range(2):
            def ld_wo(slot, jb=jb):
                wload(slot, wview(slot, [128, KC, 512]),
                      wout_d[l].rearrange("(k p) c -> p k c", p=128)[:, :, jb * 512:(jb + 1) * 512])
            sO = WSTR.get(ld_wo)
            WO = wview(sO, [128, KC, 512])
            for jj in range(4):
                jo = jb * 4 + jj
                for ti, (off, n) in enumerate(G.tiles):
                    b = P.bank()
                    for kc in range(KC):
                        MM(P, psum[b][:, 0:n], WO[:, kc, jj * 128:(jj + 1) * 128], MERGED[:, kc, off:off + n], kc == 0, kc == KC - 1,
                           [("W", sO), ("MERGED", ti)], [("B", b)])
                    STT(P, "dve", X[:, jo, off:off + n], psum[b][:, 0:n], MOD[:, l, j, 16 + jo:17 + jo], X[:, jo, off:off + n],
                        ALU.mult, ALU.add, [("B", b), ("MODV",), ("X", gn, ti)], [("X", gn, ti)])
                    P.free(b)
        barrier()
        P.stage += 1
        if P.stage >= STOP:
            return

        f6 = Frame(nc, PH0, lim, f"L{l}{gn}p6")
        norm_mod(G, X, f6, SCL[:, l, j, 8:16], MOD[:, l, j, 24:32], l)
        barrier()
        P.stage += 1
        if P.stage >= STOP:
            return
        f6 = Frame(nc, PH0, lim, f"L{l}{gn}p6b")
        ACTB = f6.alloc("ACTB", [128, NFC, T], BF16)
        Lc = G.seglen + 2
        UG = [f6.alloc(f"UG{i}", [128, G.nseg, Lc], F32) for i in range(2)]
        UV = [f6.alloc(f"UV{i}", [128, T], BF16) for i in range(2)]
        TC = [f6.alloc(f"TC{i}", [128, T], F32) for i in range(2)]
        if G.sample:
            MASK = f6.alloc("MASK", [128, TS], F32)
            DMA(P, "sp", MASK[:, :], mask_d.get(), ("tab", 3), (), [("MASK",)])
        for i in range(2):
            MEMSET(P, "pool", UG[i][:, :, :], 0.0, [("UG", i)])
        SL = G.seglen
        for bi in range(6):
            cb = bi * 512
            nb_ = min(512, D_FF - cb)

            def ld_up(slot, c0=cb, nb_=nb_):
                wload(slot, wview(slot, [128, KC, nb_]), wup_d[l].rearrange("(k p) c -> p k c", p=128)[:, :, c0:c0 + nb_])

            def ld_upv(slot, c0=cb, nb_=nb_):
                wload(slot, wview(slot, [128, KC, nb_]),
                      wup_d[l].rearrange("(k p) c -> p k c", p=128)[:, :, D_FF + c0:D_FF + c0 + nb_])
            sU = WSTR.get(ld_up)
            sV_ = WSTR.get(ld_upv)
            WU = wview(sU, [128, KC, nb_])
            WV = wview(sV_, [128, KC, nb_])
            for cc in range(nb_ // 128):
                c = bi * 4 + cc
                i2 = c % 2
                for ti, (off, n) in enumerate(G.tiles):
                    bg = proj(G, ti, mk_lhs(lambda kc, cc=cc: WU[:, kc, cc * 128:(cc + 1) * 128], [("W", sU)]), 128)
                    if G.sample:
                        TT(P, "dve", segv(UG[i2], G, ti, 1), psum[bg][:, 0:n], MASK[:, off:off + n], ALU.mult,
                           [("B", bg), ("MASK",)], [("UG", i2)])
                    else:
                        CP(P, "dve", segv(UG[i2], G, ti, 1), psv(psum[bg][:, 0:n], G, ti), [("B", bg)], [("UG", i2)])
                    P.free(bg)
                    bv = proj(G, ti, mk_lhs(lambda kc, cc=cc: WV[:, kc, cc * 128:(cc + 1) * 128], [("W", sV_)]), 128)
                    CP(P, "act", UV[i2][:, off:off + n], psum[bv][:, 0:n], [("B", bv)], [("UV", i2)])
                    P.free(bv)
                tc3 = TC[i2][:, :].rearrange("p (s l) -> p s l", l=SL)
                ug = UG[i2]
                ACT(P, tc3, ug[:, :, 0:SL], AF.Identity, [("UG", i2), ("VECS",)], [("TC", i2)],
                    bias=vec(V_CB + c), scale=vec(V_CW + c))
                STT(P, "dve", tc3, ug[:, :, 1:SL + 1], vec(V_CW + NFC + c), tc3, ALU.mult, ALU.add,
                    [("UG", i2), ("TC", i2), ("VECS",)], [("TC", i2)])
                STT(P, "dve", tc3, ug[:, :, 2:SL + 2], vec(V_CW + 2 * NFC + c), tc3, ALU.mult, ALU.add,
                    [("UG", i2), ("TC", i2), ("VECS",)], [("TC", i2)])
                ACT(P, TC[i2][:, :], TC[i2][:, :], AF.Silu, [("TC", i2)], [("TC", i2)])
                TT(P, "dve", ACTB[:, c, :], TC[i2][:, :], UV[i2][:, :], ALU.mult, [("TC", i2), ("UV", i2)], [("ACTB", c)])
        for jo in range(8):
            def ld_dn(slot, jo=jo):
                wload(slot, wview(slot, [128, NFC, 128]),
                      wdn_d[l].rearrange("(k p) c -> p k c", p=128)[:, :, jo * 128:(jo + 1) * 128])
            sDn = WSTR.get(ld_dn)
            WDN = wview(sDn, [128, NFC, 128])
            for ti, (off, n) in enumerate(G.tiles):
                b = P.bank()
                for c in range(NFC):
                    MM(P, psum[b][:, 0:n], WDN[:, c, :], ACTB[:, c, off:off + n], c == 0, c == NFC - 1,
                       [("W", sDn), ("ACTB", c)], [("B", b)])
                STT(P, "dve", X[:, jo, off:off + n], psum[b][:, 0:n], MOD[:, l, j, 40 + jo:41 + jo], X[:, jo, off:off + n],
                    ALU.mult, ALU.add, [("B", b), ("MODV",), ("X", gn, ti)], [("X", gn, ti)])
                P.free(b)
        if (not G.sample) and l == 0:
            DMA(P, "sp", xp_scr, X[:, :, :], "xp", [("X", "p", ti) for ti in range(ntile)], [("XPSCR",)])
        barrier()
        P.lastX = getattr(P, "lastX", {})
        P.lastX[gn] = X

    def final_norm(G):
        gn = G.name
        X = XS if G.sample else P.lastX.get("p", XS)
        if P.dry:
            return
        barrier()
        lim = PH_END if G.sample else XP_OFF
        fr = Frame(nc, PH0, lim, f"fin{gn}")
        SQ = fr.alloc("SQ", [128, KC, 512], BF16)
        RSTD = fr.alloc("RSTD", [128, 512], F32)
        YT = [fr.alloc(f"YT{i}", [128, 512], F32) for i in range(2)]
        for ti, (off, n) in enumerate(G.tiles):
            for kc in range(KC):
                ACT(P, SQ[:, kc, 0:n], X[:, kc, off:off + n], AF.Square, [("X", gn, ti)], [("SQ", kc)])
            rms_rstd(G, [SQ[:, kc, 0:n] for kc in range(KC)], ONES[:, :], n, 1.0 / D, RSTD[:, 0:n],
                     [("SQ", kc) for kc in range(KC)], ("RSTD",))
            if G.sample:
                lo, hi = max(off, HALO) - off, min(off + n, HALO + OWN) - off
                so = off + lo - HALO
            else:
                lo, hi, so = 0, n, off
            for kc in range(KC):
                y = YT[kc % 2]
                STT(P, "dve", y[:, 0:hi - lo], X[:, kc, off + lo:off + hi], VECS[:, 0, V_GF + kc:V_GF + kc + 1], RSTD[:, lo:hi],
                    ALU.mult, ALU.mult, [("X", gn, ti), ("RSTD",), ("VECS",)], [("YT", kc % 2)])
                dst = (ys_d if G.sample else yp_d)[:, kc, so:so + hi - lo]
                DMA(P, "sp", dst, y[:, 0:hi - lo], ("out", 6), [("YT", kc % 2)], [])

    P.dry = True
    emit_all()
    P.dry = False
    emit_all()

    dma_names = sorted(P.dma_counts.keys())
    sems = {e: es.enter_context(nc.semaphore(f"s_{e}")) for e in COMPUTE}
    dma_sems = {n_: es.enter_context(nc.semaphore(f"d_{n_}")) for n_ in dma_names}
    with nc.Block() as block:
        P.emit(block, sems, dma_sems)
    es.close()
    P.used_io = used_io
    return nc, P


def _fm(x2d):
    t = x2d.shape[0]
    return np.ascontiguousarray(x2d.reshape(t, KC, 128).transpose(2, 1, 0))


def _cols(v):
    return np.ascontiguousarray(v.reshape(-1, 128).T)


def _partner(n_half, width):
    idx = np.arange(width)
    return np.where((idx % (2 * n_half)) < n_half, idx + n_half, idx - n_half)


def _rope_table(tabs, d_head, npart_map):
    theta = np.float32(10000.0)
    da = d_head // 2
    nf = da // 2
    freqs = (theta ** (-(np.arange(0, da, 2, dtype=np.float32)) / np.float32(da))).astype(np.float32)
    row = (tabs // 64).astype(np.float32)
    col = (tabs % 64).astype(np.float32)
    ang_r = row[None, :] * freqs[:, None]
    ang_c = col[None, :] * freqs[:, None]
    cos = np.zeros((d_head, len(tabs)), np.float32)
    sin = np.zeros((d_head, len(tabs)), np.float32)
    for d in range(d_head):
        ang = (ang_r if d < da else ang_c)[d % nf]
        cos[d] = np.cos(ang).astype(np.float32)
        sg = -1.0 if (d % da) < nf else 1.0
        sin[d] = sg * np.sin(ang).astype(np.float32)
    return cos, sin


def _pool_tab(t_abs, seq_len, valid):
    out = np.zeros((4, len(t_abs)), np.float32)
    for gi, w in enumerate((2, 4, 8, 16)):
        lo = np.clip(t_abs - w // 2, 0, seq_len)
        hi = np.clip(t_abs - w // 2 + w, 0, seq_len)
        cnt = np.maximum(hi - lo, 1).astype(np.float32)
        out[gi] = np.where(valid, np.float32(1.0) / cnt, np.float32(0.0))
    return out


_CACHE = {}


def prep_inputs(x_prompt, x_sample, c, cache_mla_ckv, cache_mla_krope, cache_gqa_k, cache_gqa_v,
           c_ctx, w_ada, b_ada, g_norm_mix, w_in, g_q_a, w_q_b, g_kv_a, w_kv_b, g_q_gqa, g_k_gqa,
           w_pool, pool_scale, w_br_a, w_br_b, w_br_c, w_out, g_norm_ffn, w_up, conv_w, conv_b,
           w_down, g_final):
    f = lambda a: np.ascontiguousarray(np.asarray(a, dtype=np.float32))
    x_prompt, x_sample, c = f(x_prompt), f(x_sample), f(c)
    cache_mla_ckv, cache_mla_krope, cache_gqa_k, cache_gqa_v = f(cache_mla_ckv), f(cache_mla_krope), f(cache_gqa_k), f(cache_gqa_v)
    c_ctx = f(c_ctx)
    shared = {"w_ada": f(w_ada), "w_in": f(w_in), "w_q_b": f(w_q_b), "w_kv_b": f(w_kv_b), "w_pool": f(w_pool),
              "w_br_a": f(w_br_a), "w_br_b": f(w_br_b), "w_br_c": f(w_br_c), "w_out": f(w_out), "w_up": f(w_up),
              "w_down": f(w_down)}
    b_ada, g_norm_mix, g_q_a, g_kv_a = f(b_ada), f(g_norm_mix), f(g_q_a), f(g_kv_a)
    g_q_gqa, g_k_gqa, pool_scale, g_norm_ffn = f(g_q_gqa), f(g_k_gqa), f(pool_scale), f(g_norm_ffn)
    conv_w, conv_b, g_final = f(conv_w), f(conv_b), f(g_final)

    vecs = np.zeros((128, L, NV), np.float32)
    p64 = _partner(16, 64)
    for l in range(L):
        vecs[:, l, V_GMIX:V_GMIX + 8] = _cols(g_norm_mix[l])
        vecs[:, l, V_GFFN:V_GFFN + 8] = _cols(g_norm_ffn[l])
        vecs[:, l, V_BADA:V_BADA + 48] = _cols(b_ada[l])
        vecs[:, l, V_GQA:V_GQA + 2] = _cols(g_q_a[l])
        vecs[:, l, V_GKVA] = g_kv_a[l]
        vecs[:, l, V_GQ] = np.tile(g_q_gqa[l], 2)
        vecs[:, l, V_GQP] = np.tile(g_q_gqa[l][p64], 2)
        vecs[:, l, V_GK] = np.tile(g_k_gqa[l], 2)
        vecs[:, l, V_GKP] = np.tile(g_k_gqa[l][p64], 2)
        vecs[:, l, V_PSC:V_PSC + 4] = _cols(pool_scale[l])
        for k in range(3):
            vecs[:, l, V_CW + k * NFC:V_CW + (k + 1) * NFC] = _cols(conv_w[l, k])
        vecs[:, l, V_CB:V_CB + NFC] = _cols(conv_b[l])
        vecs[:, l, V_GF:V_GF + 8] = _cols(g_final)
    sel = np.zeros((32, 96), np.float32)
    sel[np.arange(32), 64 + np.arange(32)] = 1.0
    tp = np.arange(TP) % 256
    ptab_p = np.ascontiguousarray(np.broadcast_to(_pool_tab(tp, 256, np.ones(TP, bool))[None], (128, 4, TP)))

    in_maps = []
    for core in range(8):
        b, q = core // 4, core % 4
        a = q * OWN
        t_abs = np.arange(a - HALO, a + OWN + HALO)
        valid = (t_abs >= 0) & (t_abs < 4096)
        tcl = np.clip(t_abs, 0, 4095)
        xs = np.zeros((TS, D), np.float32)
        xs[valid] = x_sample[b, t_abs[valid]]
        cos64, sin64 = _rope_table(tcl, 64, None)
        cos32, sin32 = _rope_table(tcl, 32, None)
        rope64 = np.stack([np.tile(cos64, (2, 1)), np.tile(sin64, (2, 1))], axis=1)
        r32 = np.zeros((128, 2, TS), np.float32)
        for base in (0, 64):
            r32[base:base + 32, 0] = cos32
            r32[base:base + 32, 1] = sin32
        mask = np.ascontiguousarray(np.broadcast_to(valid.astype(np.float32)[None], (128, TS)))
        ptab_s = np.ascontiguousarray(np.broadcast_to(_pool_tab(t_abs, 4096, valid)[None], (128, 4, TS)))
        m = {
            "xp": _fm(x_prompt[4 * core:4 * core + 4].reshape(TP, D)),
            "xs": _fm(xs),
            "cvec": np.ascontiguousarray(np.stack([_cols(c_ctx), _cols(c[b])], axis=2)),
            "c_ckv": np.ascontiguousarray(cache_mla_ckv[b].transpose(0, 2, 1)),
            "c_krope": np.ascontiguousarray(cache_mla_krope[b].transpose(0, 2, 1)),
            "c_k": np.ascontiguousarray(cache_gqa_k[b].transpose(0, 2, 3, 1)),
            "c_v": np.ascontiguousarray(cache_gqa_v[b].reshape(L, 4, 128, 128).transpose(0, 2, 1, 3)),
            "vecs": vecs, "rope64": np.ascontiguousarray(rope64), "rope32": r32, "mask": mask,
            "ptab_s": ptab_s, "ptab_p": ptab_p, "sel": sel,
        }
        m.update(shared)
        in_maps.append(m)

    return in_maps


def kernel(**inputs):
    in_maps = prep_inputs(**inputs)
    if "nc" not in _CACHE:
        _CACHE["nc"], _CACHE["P"] = build_program()
    used = _CACHE["P"].used_io
    in_maps = [{k: m[k] for k in used["in"]} for m in in_maps]
    res = run_bass_kernel_spmd(_CACHE["nc"], in_maps, core_ids=list(range(8)))
    return assemble(res.results)


def assemble(results):
    oshape = {"y_p": (128, KC, TP), "y_s": (128, KC, OWN), "o_ckv": (L, 128, TP), "o_krope": (L, 32, TP),
              "o_k": (L, 2, 64, TP), "o_v": (L, 128, 8, 128)}
    R = [{k: (np.asarray(r[k]).reshape(oshape[k]) if k in r else np.zeros(oshape[k], np.float32)) for k in oshape} for r in results]

    y_prompt = np.zeros((32, 256, D), np.float32)
    y_sample = np.zeros((2, 4096, D), np.float32)
    n_ckv = np.zeros((32, L, 256, 128), np.float32)
    n_kr = np.zeros((32, L, 256, 32), np.float32)
    n_k = np.zeros((32, L, 256, 2, 64), np.float32)
    n_v = np.zeros((32, L, 256, 2, 64), np.float32)
    for core in range(len(R)):
        r = R[core]
        b, q = core // 4, core % 4
        y_prompt[4 * core:4 * core + 4] = np.asarray(r["y_p"]).transpose(2, 1, 0).reshape(4, 256, D)
        y_sample[b, q * OWN:(q + 1) * OWN] = np.asarray(r["y_s"]).transpose(2, 1, 0).reshape(OWN, D)
        n_ckv[4 * core:4 * core + 4] = np.asarray(r["o_ckv"]).reshape(L, 128, 4, 256).transpose(2, 0, 3, 1)
        n_kr[4 * core:4 * core + 4] = np.asarray(r["o_krope"]).reshape(L, 32, 4, 256).transpose(2, 0, 3, 1)
        n_k[4 * core:4 * core + 4] = np.asarray(r["o_k"]).reshape(L, 2, 64, 4, 256).transpose(3, 0, 4, 1, 2)
        ov = np.asarray(r["o_v"]).transpose(0, 2, 1, 3).reshape(L, 4, 256, 2, 64)
        n_v[4 * core:4 * core + 4] = ov.transpose(1, 0, 2, 3, 4)
    return (y_prompt, y_sample, n_ckv, n_kr, n_k, n_v)
```

```python
import contextlib
import os
import numpy as np
import concourse.bass as bass
import concourse.mybir as mybir
from concourse.bass_utils import run_bass_kernel_spmd

F32 = mybir.dt.float32
BF16 = mybir.dt.bfloat16
ALU = mybir.AluOpType
AF = mybir.ActivationFunctionType

D = 1024
KC = 8
L = 2
TP = 1024
TS = 1152
HALO = 64
OWN = 1024
S_ALL = 4608
NKB_S = 36
D_FF = 2816
NFC = 22
EPS = 1e-6
IN_COLS = 4768
C_QA, C_KV, C_KR, C_GQ, C_GK, C_GV, C_PO, C_GT = 0, 256, 384, 416, 928, 1056, 1184, 1696
NV = 176
V_GMIX, V_GFFN, V_BADA, V_GQA, V_GKVA, V_GQ, V_GQP, V_GK, V_GKP, V_PSC, V_CW, V_CB, V_GF = \
    0, 8, 16, 64, 66, 67, 68, 69, 70, 71, 75, 141, 163
SBUF_BASE = 16512
SBUF_END = 229376 - 128


class Op:
    __slots__ = ("eng", "fn", "deps", "signal", "sigval", "dma_sem", "dma_val", "idx")


class Obj:
    __slots__ = ("last_w", "readers")

    def __init__(self):
        self.last_w = None
        self.readers = []


COMPUTE = ("pe", "act", "dve", "pool")
QUEUES = ("sp",)


class Prog:
    def __init__(self, nc):
        self.nc = nc
        self.ops = {e: [] for e in COMPUTE + QUEUES}
        self.objs = {}
        self.n = 0
        self.dma_counts = {}
        self.free_banks = list(range(8))
        self.dry = False
        self.fam_ctr = {}
        self.last_on_sem = {}

    def o(self, key):
        ob = self.objs.get(key)
        if ob is None:
            ob = self.objs[key] = Obj()
        return ob

    def add(self, eng, fn, r=(), w=(), dma_sem=None, grp=False):
        if self.dry:
            return None
        fam_prev = None
        if isinstance(dma_sem, tuple):
            fam, nmem = dma_sem
            i = self.fam_ctr.get(fam, 0)
            self.fam_ctr[fam] = i + 1
            dma_sem = f"{fam}{i % nmem}"
            fam_prev = self.last_on_sem.get(dma_sem)
        op = Op()
        op.eng = eng
        op.fn = fn
        op.signal = False
        op.sigval = None
        op.dma_sem = dma_sem
        op.dma_val = None
        op.idx = self.n
        self.n += 1
        deps = {}
        is_dma = dma_sem is not None
        for k in r:
            ob = self.o(k)
            d = ob.last_w
            if d is not None:
                deps[d.idx] = d
            if isinstance(k, tuple) and k[0] == "B":
                for d in ob.readers:
                    if d.eng != eng:
                        deps[d.idx] = d
        for k in w:
            ob = self.o(k)
            d = ob.last_w
            if d is not None:
                deps[d.idx] = d
            for d in ob.readers:
                deps[d.idx] = d
        if grp:
            deps = {i: d for i, d in deps.items() if d.dma_sem != dma_sem}
        if fam_prev is not None:
            deps[fam_prev.idx] = fam_prev
        if eng == "pe":
            deps = {i: d for i, d in deps.items() if not (d.eng == "pe" and d.dma_sem is None)}
        op.deps = list(deps.values())
        for d in op.deps:
            d.signal = True
        for k in r:
            self.o(k).readers.append(op)
        for k in w:
            ob = self.o(k)
            ob.last_w = op
            ob.readers = []
        if is_dma:
            c = self.dma_counts.get(dma_sem, 0) + (1 if dma_sem.startswith("cc") else 16)
            self.dma_counts[dma_sem] = c
            op.dma_val = c
            self.last_on_sem[dma_sem] = op
        self.ops[eng].append(op)
        return op

    def bank(self):
        if self.dry:
            return 0
        assert self.free_banks, "out of PSUM banks"
        return self.free_banks.pop(0)

    def free(self, b):
        if self.dry:
            return
        self.free_banks.append(b)

    def emit(self, block, sems, dma_sems):
        for e in COMPUTE:
            c = 0
            for op in self.ops[e]:
                if op.signal and op.dma_sem is None:
                    c += 1
                    op.sigval = c
        eng_fn = {"pe": block.tensor, "act": block.scalar, "dve": block.vector, "pool": block.gpsimd,
                  "sp": block.sync}
        for e in COMPUTE + QUEUES:
            ops = self.ops[e]

            def body(eng, ops=ops, e=e):
                waited = {}
                for op in ops:
                    need = {}
                    for d in op.deps:
                        if d.dma_sem is not None:
                            key, val = ("d", d.dma_sem), d.dma_val
                        else:
                            key, val = ("c", d.eng), d.sigval
                        if need.get(key, 0) < val:
                            need[key] = val
                    for key, val in need.items():
                        if waited.get(key, 0) >= val:
                            continue
                        waited[key] = val
                        sem = dma_sems[key[1]] if key[0] == "d" else sems[key[1]]
                        eng.wait_ge(sem, val)
                    ins = op.fn(eng)
                    if op.dma_sem is not None and op.dma_sem.startswith("cc"):
                        ins.then_inc(dma_sems[op.dma_sem])
                    elif op.dma_sem is not None:
                        ins.then_inc(dma_sems[op.dma_sem], 16)
                    elif op.signal:
                        ins.then_inc(sems[e], 1)
                if e == "sp":
                    for name, cnt in self.dma_counts.items():
                        eng.wait_ge(dma_sems[name], cnt)
            eng_fn[e](body)


class Group:
    def __init__(self, name, T, tiles, nseg, seglen, sample):
        self.name, self.T, self.tiles, self.nseg, self.seglen, self.sample = name, T, tiles, nseg, seglen, sample


GP = Group("p", TP, [(0, 512), (512, 512)], 4, 256, False)
GS = Group("s", TS, [(0, 384), (384, 384), (768, 384)], 1, TS, True)


def MM(P, out, lhsT, rhs, start, stop, r, w):
    return P.add("pe", lambda e: e.matmul(out, lhsT=lhsT, rhs=rhs, start=start, stop=stop), r, w)


def ACT(P, out, in_, func, r, w, bias=None, scale=None):
    kw = {}
    if bias is not None:
        kw["bias"] = bias
    if scale is not None:
        kw["scale"] = scale
    return P.add("act", lambda e: e.activation(out=out, in_=in_, func=func, **kw), r, w)


def TT(P, eng, out, in0, in1, op, r, w):
    return P.add(eng, lambda e: e.tensor_tensor(out=out, in0=in0, in1=in1, op=op), r, w)


def STT(P, eng, out, in0, scalar, in1, op0, op1, r, w):
    return P.add(eng, lambda e: e.scalar_tensor_tensor(out=out, in0=in0, scalar=scalar, in1=in1,
                                                       op0=op0, op1=op1), r, w)


def CP(P, eng, out, in_, r, w):
    if eng == "act":
        return P.add("act", lambda e: e.activation(out=out, in_=in_, func=AF.Copy), r, w)
    return P.add(eng, lambda e: e.tensor_copy(out=out, in_=in_), r, w)


def MEMSET(P, eng, ap, val, w):
    return P.add(eng, lambda e: e.memset(ap, val), (), w)


def RECIP(P, out, in_, r, w):
    return P.add("dve", lambda e: e.reciprocal(out=out, in_=in_), r, w)


def DMA(P, q, out, in_, sem, r, w, grp=False):
    if isinstance(sem, tuple) and sem[0] == "out" and 'o' in os.environ.get('K_SKIP', ''):
        return None
    return P.add(q, lambda e: e.dma_start(out=out, in_=in_), r, w, dma_sem=sem, grp=grp)


_UID = [0]


class Frame:
    def __init__(self, nc, base, limit, tag):
        self.nc, self.cur, self.limit, self.tag = nc, base, limit, tag
        self.n = 0

    def alloc(self, name, shape, dtype):
        nbytes = int(np.prod(shape[1:])) * (4 if dtype == F32 else 2)
        nbytes = (nbytes + 63) // 64 * 64
        off = self.cur
        self.cur += nbytes
        assert self.cur <= self.limit, f"SBUF frame {self.tag} overflow at {name}: {self.cur} > {self.limit}"
        self.n += 1
        _UID[0] += 1
        return self.nc.alloc_sbuf_tensor_at(f"{self.tag}_{name}_{off}_{_UID[0]}", list(shape), dtype, offset=off)


def build_program(only=None):
    if os.environ.get('K_LAZY') == '1' and only is None:
        _, P0 = build_program(only=())
        return build_program(only=tuple(P0.used_io['in'] + P0.used_io['out'] + ['y_p']))
    STOP = int(os.environ.get('K_STOP', '1000'))
    SUB = int(os.environ.get('K_SUB', '1000'))
    SKIP = set(os.environ.get('K_SKIP', ''))
    nc = bass.Bass("TRN2", target_bir_lowering=False)

    used_io = {"in": [], "out": []}

    class Lz:
        def __init__(self, name, shape, kind):
            self.name, self.shape, self.kind, self._ap = name, list(shape), kind, None

        def get(self):
            if self._ap is None:
                self._ap = nc.dram_tensor(self.name, self.shape, F32,
                                          kind="ExternalInput" if self.kind == "in" else "ExternalOutput").ap()
                used_io[self.kind].append(self.name)
            return self._ap

        def __getitem__(self, k):
            return self.get()[k]

        def rearrange(self, *a, **kw):
            return self.get().rearrange(*a, **kw)

    LAZY = only == ()

    def din(name, shape, dt=F32):
        z = Lz(name, shape, "in")
        if not LAZY and (only is None or name in only):
            z.get()
        return z

    def dout(name, shape, dt=F32):
        z = Lz(name, shape, "out")
        if not LAZY and (only is None or name in only):
            z.get()
        return z

    xp_d = din("xp", [128, KC, TP])
    xs_d = din("xs", [128, KC, TS])
    cvec_d = din("cvec", [128, KC, 2])
    cckv_d = din("c_ckv", [L, 128, 512])
    ckr_d = din("c_krope", [L, 32, 512])
    ck_d = din("c_k", [L, 2, 64, 512])
    cv_d = din("c_v", [L, 128, 4, 128])
    wada_d = din("w_ada", [L, D, 6 * D])
    win_d = din("w_in", [L, D, IN_COLS])
    wqb_d = din("w_q_b", [L, 256, 768])
    wkvb_d = din("w_kv_b", [L, 128, 1024])
    wpool_d = din("w_pool", [L, 4, 128, 128])
    wbr_d = [din("w_br_a", [L, 512, D]), din("w_br_b", [L, 512, D]), din("w_br_c", [L, 512, D])]
    wout_d = din("w_out", [L, D, D])
    wup_d = din("w_up", [L, D, 2 * D_FF])
    wdn_d = din("w_down", [L, D_FF, D])
    vecs_d = din("vecs", [128, L, NV])
    rope64_d = din("rope64", [128, 2, TS])
    rope32_d = din("rope32", [128, 2, TS])
    mask_d = din("mask", [128, TS])
    ptab_d = {"s": din("ptab_s", [128, 4, TS]), "p": din("ptab_p", [128, 4, TP])}
    sel_d = din("sel", [32, 96])

    yp_d = dout("y_p", [128, KC, TP])
    ys_d = dout("y_s", [128, KC, OWN])
    ockv_d = dout("o_ckv", [L, 128, TP])
    okr_d = dout("o_krope", [L, 32, TP])
    ok_d = dout("o_k", [L, 2, 64, TP])
    ov_d = dout("o_v", [L, 128, 8, 128])

    xp_scr = nc.dram_tensor("xp_scr", [128, KC, TP], F32).ap()
    send_t = [nc.dram_tensor(f"send{l}", [416, OWN], BF16) for l in range(L)]
    recv_t = [nc.dram_tensor(f"recv{l}", [4 * 416, OWN], BF16) for l in range(L)]

    P = Prog(nc)
    P.bar_deps = []
    P.bar_need = set()

    orig_add = P.add

    def add(eng, fn, r=(), w=(), dma_sem=None, nobar=False, grp=False):
        if P.dry:
            return None
        op = orig_add(eng, fn, r, w, dma_sem, grp)
        if eng in P.bar_need and not nobar:
            P.bar_need.discard(eng)
            have = {d.idx for d in op.deps}
            for d in P.bar_deps:
                if d.idx not in have and d is not op:
                    if eng == "pe" and d.eng == "pe" and d.dma_sem is None:
                        continue
                    op.deps.append(d)
                    d.signal = True
        return op

    P.add = add
    P.last_dma = {}

    def barrier():
        if P.dry:
            return
        deps = []
        for e in COMPUTE + QUEUES:
            for op in reversed(P.ops[e]):
                if op.dma_sem is None:
                    deps.append(op)
                    break
        for e in COMPUTE + QUEUES:
            seen = set()
            for op in reversed(P.ops[e]):
                if op.dma_sem is not None and op.dma_sem not in seen:
                    seen.add(op.dma_sem)
                    deps.append(op)
        P.bar_deps = deps
        P.bar_need = set(COMPUTE + QUEUES)

    fx = Frame(nc, SBUF_BASE, SBUF_END, "fx")
    XS = fx.alloc("XS", [128, KC, TS], F32)
    H = fx.alloc("H", [128, KC, TS], BF16)
    ONES = fx.alloc("ONES", [128, 128], BF16)
    BD = fx.alloc("BD", [128, 128], BF16)
    SEL = fx.alloc("SEL", [128, 96], BF16)
    VECS = fx.alloc("VECS", [128, L, NV], F32)
    CV = fx.alloc("CV", [128, KC, 2], F32)
    CSIL = fx.alloc("CSIL", [128, KC, 2], F32)
    MOD = fx.alloc("MOD", [128, L, 2, 48], F32)
    SCL = fx.alloc("SCL", [128, L, 2, 16], F32)
    EPSV = fx.alloc("EPSV", [128, 1], F32)
    NSLOT = 5
    WS = [fx.alloc(f"W{i}", [128, 4096], BF16) for i in range(NSLOT)]
    PH0 = fx.cur
    PH_END = SBUF_END
    XP_OFF = PH_END - KC * TP * 4

    psum = []
    es = contextlib.ExitStack()
    for b in range(8):
        psum.append(es.enter_context(nc.psum_tensor(f"ps{b}", [128, 512], F32)))

    def wview(slot, shape):
        _UID[0] += 1
        n = int(np.prod(shape[1:]))
        return WS[slot][:, 0:n].rearrange(
            "p (a b c) -> p a b c" if len(shape) == 4 else ("p (a b) -> p a b" if len(shape) == 3 else "p a -> p a"),
            **({"a": shape[1], "b": shape[2]} if len(shape) == 4 else ({"a": shape[1]} if len(shape) == 3 else {})))

    class WStream:
        def __init__(self):
            self.reqs, self.pos, self.issued, self.depth = [], 0, 0, NSLOT - 2

        def get(self, loader):
            if P.dry:
                self.reqs.append(loader)
                return 0
            i = self.pos
            self.pos += 1
            while self.issued < min(len(self.reqs), i + self.depth + 1):
                k = self.issued
                self.reqs[k](k % NSLOT)
                self.issued += 1
            return i % NSLOT

    WSTR = WStream()

    def wload(slot, dst_ap, src_ap):
        P.add("pool", lambda e: e.dma_start(out=dst_ap, in_=src_ap), (), [("W", slot)],
              dma_sem=f"w{slot}", nobar=True, grp=True)

    def win_block(l, c0, nc_):
        def ld(slot):
            wload(slot, wview(slot, [128, KC, nc_]),
                  win_d[l].rearrange("(k p) c -> p k c", p=128)[:, :, c0:c0 + nc_])
        return ld

    def tcols(G, ti):
        off, n = G.tiles[ti]
        return off, n

    def segv(buf, G, ti, pad, shift=0):
        off, n = G.tiles[ti]
        if G.nseg == 1:
            return buf[:, 0, pad + shift + off: pad + shift + off + n]
        s0 = off // G.seglen
        ns = n // G.seglen
        return buf[:, s0:s0 + ns, pad + shift: pad + shift + G.seglen]

    def psv(ap, G, ti):
        if G.nseg == 1:
            return ap
        return ap.rearrange("p (s l) -> p s l", l=G.seglen)

    def proj(G, ti, lhs_of_kc, M, Hb=None):
        off, n = G.tiles[ti]
        b = P.bank()
        for kc in range(KC):
            MM(P, psum[b][0:M, 0:n], lhs_of_kc(kc), H[:, kc, off:off + n], kc == 0, kc == KC - 1,
               [("H", G.name, ti)] + lhs_of_kc.deps, [("B", b)])
        return b

    def mk_lhs(fn, deps):
        fn.deps = deps
        return fn

    def rms_rstd(G, sq_aps, lhsT, n, inv_cnt, out_rstd, rkeys, wkey, M=128):
        b = P.bank()
        for i, sq in enumerate(sq_aps):
            MM(P, psum[b][0:M, 0:n], lhsT, sq, i == 0, i == len(sq_aps) - 1, rkeys + [("CONST",)], [("B", b)])
        ACT(P, out_rstd, psum[b][0:M, 0:n], AF.Ln, [("B", b), ("CONST",)], [wkey], bias=EPSV[0:M, 0:1], scale=inv_cnt)
        ACT(P, out_rstd, out_rstd, AF.Exp, [wkey], [wkey], scale=-0.5)
        P.free(b)

    def norm_mod(G, X, fr, scale_ap, shift_ap, l):
        SQ = fr.alloc("SQ", [128, KC, 512], BF16)
        RSTD = fr.alloc("RSTD", [128, 512], F32)
        TM = [fr.alloc(f"TM{i}", [128, 512], F32) for i in range(2)]
        for ti, (off, n) in enumerate(G.tiles):
            for kc in range(KC):
                ACT(P, SQ[:, kc, 0:n], X[:, kc, off:off + n], AF.Square, [("X", G.name, ti)], [("SQ", kc)])
            rms_rstd(G, [SQ[:, kc, 0:n] for kc in range(KC)], ONES[:, :], n, 1.0 / D, RSTD[:, 0:n],
                     [("SQ", kc) for kc in range(KC)], ("RSTD",))
            for kc in range(KC):
                t = TM[kc % 2]
                STT(P, "dve", t[:, 0:n], X[:, kc, off:off + n], scale_ap[:, kc:kc + 1], RSTD[:, 0:n],
                    ALU.mult, ALU.mult, [("X", G.name, ti), ("RSTD",), ("MODV",)], [("TM", kc % 2)])
                if shift_ap is None:
                    CP(P, "pool", H[:, kc, off:off + n], t[:, 0:n], [("TM", kc % 2)], [("H", G.name, ti)])
                else:
                    ACT(P, H[:, kc, off:off + n], t[:, 0:n], AF.Identity, [("TM", kc % 2), ("MODV",)], [("H", G.name, ti)],
                        bias=shift_ap[:, kc:kc + 1])

    def emit_all():
        WSTR.pos = 0
        MEMSET(P, "dve", ONES[:, :], 1.0, [("CONST",)])
        MEMSET(P, "dve", BD[:, :], 0.0, [("CONST",)])
        MEMSET(P, "dve", BD[0:64, 0:64], 1.0, [("CONST",)])
        MEMSET(P, "dve", BD[64:128, 64:128], 1.0, [("CONST",)])
        MEMSET(P, "dve", EPSV[:, :], EPS, [("CONST",)])
        P.add("pool", lambda e: e.dma_start(out=SEL[0:32, :], in_=sel_d.get()), (), [("CONST",)], dma_sem="miscp")
        DMA(P, "sp", VECS[:, :, :], vecs_d.get(), "vecs", (), [("VECS",)])
        DMA(P, "sp", CV[:, :, :], cvec_d.get(), "cv", (), [("CV",)])
        DMA(P, "sp", XS[:, :, :], xs_d.get(), "xs", (), [("X", "s", ti) for ti in range(3)])
        ACT(P, CSIL[:, :, :], CV[:, :, :], AF.Silu, [("CV",)], [("CSIL",)])
        fa = Frame(nc, PH0, PH_END, "ada")
        WA = [fa.alloc(f"WA{i}", [128, KC, 512], F32) for i in range(3)]
        nblk = 0
        for l in range(L):
            b = P.bank()
            for cb in range(12):
                s = nblk % 3
                nblk += 1
                DMA(P, "sp", WA[s][:, :, :], wada_d[l].rearrange("(k p) c -> p k c", p=128)[:, :, cb * 512:(cb + 1) * 512],
                    f"wa{s}", (), [("WA", s)])
                for cc in range(4):
                    c48 = cb * 4 + cc
                    for kc in range(KC):
                        MM(P, psum[b][:, 2 * c48:2 * c48 + 2], WA[s][:, kc, cc * 128:(cc + 1) * 128], CSIL[:, kc, :],
                           kc == 0, kc == KC - 1, [("WA", s), ("CSIL",)], [("B", b)])
            for j in range(2):
                TT(P, "dve", MOD[:, l, j, :], psum[b][:, 0:96].rearrange("p (c j) -> p c j", j=2)[:, :, j],
                   VECS[:, l, V_BADA:V_BADA + 48], ALU.add, [("B", b), ("VECS",)], [("MODV",)])
                STT(P, "dve", SCL[:, l, j, 0:8], MOD[:, l, j, 8:16], 1.0, VECS[:, l, V_GMIX:V_GMIX + 8],
                    ALU.add, ALU.mult, [("MODV",), ("VECS",)], [("MODV",)])
                STT(P, "dve", SCL[:, l, j, 8:16], MOD[:, l, j, 32:40], 1.0, VECS[:, l, V_GFFN:V_GFFN + 8],
                    ALU.add, ALU.mult, [("MODV",), ("VECS",)], [("MODV",)])
            P.free(b)
        barrier()
        if 'r' in SKIP and not P.dry:
            P.free_banks = [b_ for b_ in P.free_banks if b_ not in (0, 1)]
        P.stage = 0
        for l in range(L):
            for G_ in ((GP,) if os.environ.get('K_NOSAMPLE') else (GP, GS)):
                layer(G_, l)
                if P.stage >= STOP:
                    return
                if l == L - 1 and G_ is GP:
                    final_norm(GP)
        if not os.environ.get('K_NOSAMPLE'):
            final_norm(GS)

    def layer(G, l):
        gn = G.name
        j = 1 if G.sample else 0
        T = G.T
        ntile = len(G.tiles)
        lim = PH_END if G.sample else XP_OFF
        if G.sample:
            X = XS
        else:
            _UID[0] += 1
            X = nc.alloc_sbuf_tensor_at(f"XP_{l}_{_UID[0]}", [128, KC, TP], F32, offset=XP_OFF)
            src = xp_d.get() if l == 0 else xp_scr
            DMA(P, "sp", X[:, :, :], src, "xp", [("XPSCR",)], [("X", "p", ti) for ti in range(ntile)])
        vec = lambda c0, n=1: VECS[:, l, c0:c0 + n]
        fr = Frame(nc, PH0, lim, f"L{l}{gn}")
        QAN = fr.alloc("QAN", [128, 2, T], BF16)
        QG = fr.alloc("QG", [128, 4, T], BF16)
        AOUT = fr.alloc("AOUT", [128, 4, T], BF16)
        BOUT = fr.alloc("BOUT", [128, 4, T], BF16)
        COUT = fr.alloc("COUT", [128, 4, T], BF16)
        S0 = fr.cur
        ROPE32 = None
        if G.sample:
            _UID[0] += 1
            ROPE32 = nc.alloc_sbuf_tensor_at(f"ROPE32_{l}_{_UID[0]}", [128, 2, TS], F32, offset=S0 - 4 * T * 2)
            DMA(P, "sp", ROPE32[:, :, :], rope32_d.get(), ("tab", 3), (), [("ROPE32",)])

        f1 = Frame(nc, S0, lim, f"L{l}{gn}p1")
        norm_mod(G, X, f1, SCL[:, l, j, 0:8], MOD[:, l, j, 0:8], l)
        barrier()
        P.stage += 1
        if P.stage >= STOP:
            return
        f1 = Frame(nc, S0, lim, f"L{l}{gn}p1b")
        if not G.sample:
            CKVT = f1.alloc("CKVT", [128, T], BF16)
            KROPET = f1.alloc("KROPET", [128, T], BF16)
            KDUP = f1.alloc("KDUP", [128, 2, T], BF16)
            VAUG = f1.alloc("VAUG", [128, T // 128, 2, 192], BF16)
            if 'm' not in SKIP:
                MEMSET(P, "pool", VAUG[:, :, :, 64:128], 1.0, [("VAUG", "a")])
        kv_end = f1.cur
        ZQ = f1.alloc("ZQ", [128, 2, 512], F32)
        SQ2 = f1.alloc("SQ2", [128, 2, 512], BF16)
        T1 = f1.alloc("T1", [128, 512], F32)
        T2 = f1.alloc("T2", [128, 512], F32)
        RS2 = f1.alloc("RS2", [128, 512], F32)
        KVF = [f1.alloc(f"KVF{i}", [128, 512], F32) for i in range(2)]
        SQ3 = f1.alloc("SQ3", [128, 512], BF16)
        RS3 = f1.alloc("RS3", [128, 512], F32)
        WKD = f1.alloc("WKD", [128, KC, 2, 128], BF16)
        if G.sample:
            ROPE64 = f1.alloc("ROPE64", [128, 2, TS], F32)
            DMA(P, "sp", ROPE64[:, :, :], rope64_d.get(), ("tab", 3), (), [("ROPE64",)])
            WPM = f1.alloc("WPM", [128, KC, 512], BF16)
            WKDP = f1.alloc("WKDP", [128, KC, 2, 128], BF16)
            SCKV = f1.alloc("SCKV", [128, OWN], BF16)
            SKR = f1.alloc("SKR", [128, OWN], BF16)
            SKG = f1.alloc("SKG", [128, 2, OWN], BF16)
            SV = f1.alloc("SV", [128, 8, 128], BF16)
        else:
            pass
        def own(ti):
            off, n = G.tiles[ti]
            lo, hi = max(off, HALO), min(off + n, HALO + OWN)
            return lo - off, hi - off, lo - HALO

        sA = WSTR.get(win_block(l, 0, 416))
        WA_ = wview(sA, [128, KC, 416])
        if 'P' in SKIP:
            P.stage = 10 ** 6
            return
        if G.sample:
            for half in range(2):
                CP(P, "pool", WPM[:, :, 0:32].rearrange("p k (a h e) -> p k a h e", a=2, h=2)[:, :, :, half, :],
                   WA_[:, :, C_KR:C_KR + 32].rearrange("p k (a h e) -> p k a h e", a=2, h=2)[:, :, :, 1 - half, :],
                   [("W", sA)], [("WPM",)])
        for ti, (off, n) in enumerate(G.tiles):
            for c in range(0 if 'q' in SKIP else 2):
                b = proj(G, ti, mk_lhs(lambda kc, c=c: WA_[:, kc, c * 128:(c + 1) * 128], [("W", sA)]), 128)
                if 'Q' in SKIP:
                    CP(P, "dve", ZQ[:, c, 0:n], psum[b][:, 0:n], [("B", b)], [("ZQ", c)])
                    P.free(b)
                    continue
                ACT(P, SQ2[:, c, 0:n], psum[b][:, 0:n], AF.Square, [("B", b)], [("SQ2", c)])
                CP(P, "dve", ZQ[:, c, 0:n], psum[b][:, 0:n], [("B", b)] + ([("SQ2", c)] if 'T' in SKIP else []), [("ZQ", c)])
                P.free(b)
            if 'Q' in SKIP:
                continue
            if 'S' in SKIP:
                continue
            if 'q' not in SKIP:
                rms_rstd(G, [SQ2[:, c, 0:n] for c in range(2)], ONES[:, :], n, 1.0 / 256, RS2[:, 0:n],
                         [("SQ2", 0), ("SQ2", 1)], ("RS2",))
            if 'R' in SKIP:
                continue
            for c in range(0 if 'q' in SKIP else 2):
                STT(P, "dve", QAN[:, c, off:off + n], ZQ[:, c, 0:n], vec(V_GQA + c), RS2[:, 0:n], ALU.mult, ALU.mult,
                    [("ZQ", c), ("RS2",), ("VECS",)], [("QAN", ti)])
            if 'c' in SKIP:
                continue
            if 'b' in SKIP:
                barrier()
            b = proj(G, ti, mk_lhs(lambda kc: WA_[:, kc, C_KV:C_KV + 128], [("W", sA)]), 128)
            if 'v' in SKIP:
                ACT(P, SQ3[:, 0:n], psum[b][:, 0:n], AF.Square, [("B", b)], [("SQ3",)])
                rms_rstd(G, [SQ3[:, 0:n]], ONES[:, :], n, 1.0 / 128, RS3[:, 0:n], [("SQ3",)], ("RS3",))
                STT(P, "dve", KVF[0][:, 0:n], psum[b][:, 0:n], vec(V_GKVA), RS3[:, 0:n], ALU.mult, ALU.mult,
                    [("B", b), ("RS3",), ("VECS",)], [("KVF", 0)])
                P.free(b)
                continue
            ACT(P, SQ3[:, 0:n], psum[b][:, 0:n], AF.Square, [("B", b)], [("SQ3",)])
            rms_rstd(G, [SQ3[:, 0:n]], ONES[:, :], n, 1.0 / 128, RS3[:, 0:n], [("SQ3",)], ("RS3",))
            RS2_ = RS3
            if G.sample:
                lo, hi, so = own(ti)
                STT(P, "dve", SCKV[:, so:so + hi - lo], psum[b][:, lo:hi], vec(V_GKVA), RS3[:, lo:hi], ALU.mult, ALU.mult,
                    [("B", b), ("RS3",), ("VECS",)], [("SCKV",)])
            else:
                kf = KVF[0]
                if 'x' in SKIP:
                    CP(P, "dve", ZQ[:, 0, 0:n], psum[b][:, 0:n], [("B", b)], [("ZQ", 0)])
                    STT(P, "dve", kf[:, 0:n], ZQ[:, 0, 0:n], vec(V_GKVA), RS2[:, 0:n], ALU.mult, ALU.mult,
                        [("ZQ", 0), ("RS2",), ("VECS",)], [("KVF", 0)])
                else:
                    STT(P, "dve", kf[:, 0:n], psum[b][:, 0:n], vec(V_GKVA), RS3[:, 0:n], ALU.mult, ALU.mult,
                        [("B", b), ("RS3",), ("VECS",)], [("KVF", 0)])
                DMA(P, "sp", ockv_d[l][:, off:off + n], kf[:, 0:n], ("out", 6), [("KVF", 0)], [])
                if 'y' in SKIP:
                    CP(P, "pool", CKVT[:, off:off + n], kf[:, 0:n], [("KVF", 0)], [("CKVT", "a")])
                elif 'z' not in SKIP:
                    CP(P, "act", CKVT[:, off:off + n], kf[:, 0:n], [("KVF", 0)], [("CKVT", "a")])
            P.free(b)
            if 'k' in SKIP:
                continue
            b = proj(G, ti, mk_lhs(lambda kc: WA_[:, kc, C_KR:C_KR + 32], [("W", sA)]), 32)
            if G.sample:
                b2 = proj(G, ti, mk_lhs(lambda kc: WPM[:, kc, 0:32], [("WPM",)]), 32)
                lo, hi, so = own(ti)
                w_ = hi - lo
                TT(P, "dve", T1[0:32, 0:w_], psum[b][0:32, lo:hi], ROPE32[0:32, 0, off + lo:off + hi], ALU.mult,
                   [("B", b), ("ROPE32",)], [("T1",)])
                TT(P, "dve", T2[0:32, 0:w_], psum[b2][0:32, lo:hi], ROPE32[0:32, 1, off + lo:off + hi], ALU.mult,
                   [("B", b2), ("ROPE32",)], [("T2",)])
                TT(P, "pool", SKR[0:32, so:so + w_], T1[0:32, 0:w_], T2[0:32, 0:w_], ALU.add, [("T1",), ("T2",)], [("SKR",)])
                P.free(b2)
            else:
                kf = KVF[1]
                CP(P, "act", kf[0:32, 0:n], psum[b][0:32, 0:n], [("B", b)], [("KVF", 1)])
                DMA(P, "sp", okr_d[l][:, off:off + n], kf[0:32, 0:n], ("out", 6), [("KVF", 1)], [])
                CP(P, "pool", KROPET[0:32, off:off + n], kf[0:32, 0:n], [("KVF", 1)], [("KROPET", "a")])
            P.free(b)

        if SUB <= 1:
            P.stage = 10 ** 6
            return
        sC = WSTR.get(win_block(l, C_GK, 256))
        WC_ = wview(sC, [128, KC, 256])
        for g in range(2):
            for dup in range(2):
                CP(P, "pool", WKD[:, :, g, dup * 64:(dup + 1) * 64], WC_[:, :, g * 64:(g + 1) * 64], [("W", sC)], [("WKD",)])
        if G.sample:
            for half in range(2):
                CP(P, "pool", WKDP.rearrange("p k g (a h e) -> p k (g a) h e", a=4, h=2)[:, :, :, half, :],
                   WKD.rearrange("p k g (a h e) -> p k (g a) h e", a=4, h=2)[:, :, :, 1 - half, :], [("WKD",)], [("WKDP",)])

        if SUB <= 2:
            P.stage = 10 ** 6
            return

        def headnorm_rope(b, b2, n, gv, gpv, out_ap, lo, hi, off, rk, wk):
            ACT(P, SQ2[:, 0, 0:n], psum[b][:, 0:n], AF.Square, [("B", b)], [("SQ2", 0)])
            rms_rstd(G, [SQ2[:, 0, 0:n]], BD[:, :], n, 1.0 / 64, RS2[:, 0:n], [("SQ2", 0)], ("RS2",))
            w_ = hi - lo
            if b2 is None:
                STT(P, "dve", out_ap, psum[b][:, lo:hi], gv, RS2[:, lo:hi], ALU.mult, ALU.mult,
                    [("B", b), ("RS2",), ("VECS",)] + rk, wk)
            else:
                STT(P, "dve", T1[:, 0:w_], psum[b][:, lo:hi], gv, ROPE64[:, 0, off + lo:off + hi], ALU.mult, ALU.mult,
                    [("B", b), ("ROPE64",), ("VECS",)], [("T1",)])
                STT(P, "dve", T2[:, 0:w_], psum[b2][:, lo:hi], gpv, ROPE64[:, 1, off + lo:off + hi], ALU.mult, ALU.mult,
                    [("B", b2), ("ROPE64",), ("VECS",)], [("T2",)])
                TT(P, "pool", T1[:, 0:w_], T1[:, 0:w_], T2[:, 0:w_], ALU.add, [("T1",), ("T2",)], [("T1",)])
                TT(P, "dve", out_ap, T1[:, 0:w_], RS2[:, lo:hi], ALU.mult, [("T1",), ("RS2",)] + rk, wk)

        for ti, (off, n) in enumerate(G.tiles):
            for g in range(2):
                b = proj(G, ti, mk_lhs(lambda kc, g=g: WKD[:, kc, g, :], [("WKD",)]), 128)
                if G.sample:
                    b2 = proj(G, ti, mk_lhs(lambda kc, g=g: WKDP[:, kc, g, :], [("WKDP",)]), 128)
                    lo, hi, so = own(ti)
                    headnorm_rope(b, b2, n, vec(V_GK), vec(V_GKP), SKG[:, g, so:so + hi - lo], lo, hi, off, [], [("SKG",)])
                    P.free(b2)
                else:
                    kf = KVF[g]
                    headnorm_rope(b, None, n, vec(V_GK), None, kf[:, 0:n], 0, n, off, [], [("KVF", g)])
                    DMA(P, "sp", ok_d[l][g][:, off:off + n], kf[0:64, 0:n], ("out", 6), [("KVF", g)], [])
                    CP(P, "act", KDUP[:, g, off:off + n], kf[:, 0:n], [("KVF", g)], [("KDUP", "a")])
                P.free(b)
        if SUB <= 3:
            P.stage = 10 ** 6
            return
        nblk = 8
        for i in range(nblk):
            c0 = (HALO if G.sample else 0) + i * 128
            tis = sorted({ti for ti, (off, n) in enumerate(G.tiles) if off < c0 + 128 and off + n > c0})
            b = P.bank()
            for kc in range(KC):
                MM(P, psum[b][:, 0:128], H[:, kc, c0:c0 + 128], WC_[:, kc, 128:256], kc == 0, kc == KC - 1,
                   [("H", gn, ti) for ti in tis] + [("W", sC)], [("B", b)])
            if G.sample:
                CP(P, "act", SV[:, i, :], psum[b][:, 0:128], [("B", b)], [("SV",)])
            else:
                kf = KVF[i % 2]
                CP(P, "act", kf[:, 0:128], psum[b][:, 0:128], [("B", b)], [("KVF", i % 2)])
                DMA(P, "sp", ov_d[l][:, i, :], kf[:, 0:128], ("out", 6), [("KVF", i % 2)], [])
                for o2 in (0, 128):
                    CP(P, "dve", VAUG[:, i, :, o2:o2 + 64], psum[b][:, 0:128].rearrange("p (g d) -> p g d", g=2),
                       [("B", b)], [("VAUG", "a")])
            P.free(b)

        if G.sample:
            snd, rcv = send_t[l].ap(), recv_t[l].ap()
            DMA(P, "sp", snd[0:128, :], SCKV[:, :], "snd", [("SCKV",)], [("SEND",)], grp=True)
            for g in range(2):
                DMA(P, "sp", snd[128 + 64 * g:192 + 64 * g, :], SKG[0:64, g, :], "snd", [("SKG",)], [("SEND",)], grp=True)
            DMA(P, "sp", snd[256:288, :], SKR[0:32, :], "snd", [("SKR",)], [("SEND",)], grp=True)
            DMA(P, "sp", snd[288:416, :].rearrange("p (b c) -> p b c", c=128), SV[:, :, :], "snd", [("SV",)], [("SEND",)], grp=True)
            P.add("pool", lambda e: e.collective_compute("AllGather", ALU.bypass,
                                                         replica_groups=[[0, 1, 2, 3], [4, 5, 6, 7]],
                                                         ins=[snd], outs=[rcv]),
                  [("SEND",)], [("RECV",)], dma_sem=f"cc{l}")

        if SUB <= 4:
            P.stage = 10 ** 6
            return
        sB = WSTR.get(win_block(l, C_GQ, 512))
        WB_ = wview(sB, [128, KC, 512])
        if G.sample:
            for half in range(2):
                CP(P, "pool", WPM.rearrange("p k (a h e) -> p k a h e", a=16, h=2)[:, :, :, half, :],
                   WB_.rearrange("p k (a h e) -> p k a h e", a=16, h=2)[:, :, :, 1 - half, :], [("W", sB)], [("WPM",)])
        for ti, (off, n) in enumerate(G.tiles):
            for c in range(4):
                b = proj(G, ti, mk_lhs(lambda kc, c=c: WB_[:, kc, c * 128:(c + 1) * 128], [("W", sB)]), 128)
                b2 = None
                if G.sample:
                    b2 = proj(G, ti, mk_lhs(lambda kc, c=c: WPM[:, kc, c * 128:(c + 1) * 128], [("WPM",)]), 128)
                headnorm_rope(b, b2, n, vec(V_GQ), vec(V_GQP), QG[:, c, off:off + n], 0, n, off, [], [("QG", ti)])
                if b2 is not None:
                    P.free(b2)
                P.free(b)
        barrier()
        P.stage += 1
        if P.stage >= STOP:
            return

        f2 = Frame(nc, (S0 if G.sample else kv_end), lim, f"L{l}{gn}p2")
        if G.sample:
            SK = S_ALL
            CKVT = f2.alloc("CKVT", [128, SK], BF16)
            KROPET = f2.alloc("KROPET", [128, SK], BF16)
            rv = recv_t[l].ap().rearrange("(r w) c -> w r c", w=416)
            DMA(P, "sp", CKVT[:, 0:4096].rearrange("p (r c) -> p r c", r=4), rv[0:128], "k1", [("RECV",)], [("CKVT", "a")])
            DMA(P, "sp", KROPET[0:32, 0:4096].rearrange("p (r c) -> p r c", r=4), rv[256:288], "k2", [("RECV",)], [("KROPET", "a")])
            P.add("pool", lambda e: e.dma_start(out=CKVT[:, 4096:SK], in_=cckv_d[l]), (), [("CKVT", "c")], dma_sem="k1c")
            P.add("pool", lambda e: e.dma_start(out=KROPET[0:32, 4096:SK], in_=ckr_d[l]), (), [("KROPET", "c")], dma_sem="k2c")
        else:
            SK = T
        st0 = f2.cur
        nkb = SK // 128
        if G.sample:
            KDUP = f2.alloc("KDUP", [128, 2, SK], BF16)
            VAUG = f2.alloc("VAUG", [128, nkb, 2, 192], BF16)
            for g in range(2):
                for hf in range(2):
                    DMA(P, "sp", KDUP[64 * hf:64 * hf + 64, g, 0:4096].rearrange("p (r c) -> p r c", r=4),
                        rv[128 + 64 * g:192 + 64 * g], "k3", [("RECV",)], [("KDUP", "a")], grp=True)
                    P.add("pool", lambda e, g=g, hf=hf: e.dma_start(out=KDUP[64 * hf:64 * hf + 64, g, 4096:SK], in_=ck_d[l][g]),
                          (), [("KDUP", "c")], dma_sem="k3c", grp=True)
            MEMSET(P, "pool", VAUG[:, :, :, 64:128], 1.0, [("VAUG", "a")])
            for o2 in (0, 128):
                for r_ in range(4):
                    DMA(P, "sp", VAUG.rearrange("p b g c -> p (b g) c")[:, r_ * 16:(r_ + 1) * 16, o2:o2 + 64],
                        rv[288:416, r_, :].rearrange("p (bg d) -> p bg d", d=64), "k4", [("RECV",)], [("VAUG", "a")], grp=True)
                P.add("pool", lambda e, o2=o2: e.dma_start(out=VAUG.rearrange("p b g c -> p (b g) c")[:, 64:72, o2:o2 + 64],
                                                           in_=cv_d[l].rearrange("p b (g d) -> p (b g) d", g=2)),
                      (), [("VAUG", "c")], dma_sem="k4c", grp=True)
        fpt = Frame(nc, (st0 + 46080 if G.sample else f2.cur), lim, f"L{l}{gn}pt")
        PT = [fpt.alloc(f"PT{i}", [128, 512], BF16) for i in range(4)]
        RCP = fpt.alloc("RCP", [128, 512], F32)
        pt_i = [0]

        if G.sample:
            qranges = [(off, n, 0, nkb) for (off, n) in G.tiles]
        else:
            qranges = [(s * 256, 256, 2 * s, 2) for s in range(4)]

        def attend(q_of, k_of, v_of, Kp, p0, scale, out_ap_of, rq, rk, rv_, wout):
            for (off, n, kb0, nk) in qranges:
                bo = P.bank()
                pend = []

                def qk(kb):
                    bs = P.bank()
                    MM(P, psum[bs][:, 0:n], k_of(kb), q_of(off, n), True, True, rq + rk, [("B", bs)])
                    return bs

                def pv(kb, bs, first, last):
                    pi = pt_i[0] % 4
                    pt_i[0] += 1
                    ACT(P, PT[pi][:, 0:n], psum[bs][:, 0:n], AF.Exp, [("B", bs)], [("PT", pi)], scale=scale)
                    P.free(bs)
                    MM(P, psum[bo][:, 0:n], v_of(kb), PT[pi][:, 0:n], first, last, [("PT", pi)] + rv_, [("B", bo)])

                kbs = list(range(kb0, kb0 + nk))
                DEPTH = 6
                for idx in range(len(kbs)):
                    pend.append((kbs[idx], qk(kbs[idx])))
                    if len(pend) > DEPTH:
                        kb, bs = pend.pop(0)
                        pv(kb, bs, kb == kbs[0], False)
                while pend:
                    kb, bs = pend.pop(0)
                    pv(kb, bs, kb == kbs[0], len(pend) == 0)
                s0_ = 64 - p0
                RECIP(P, RCP[p0:p0 + 64, 0:n], psum[bo][s0_:s0_ + 64, 0:n], [("B", bo)], [("RCP", p0)])
                TT(P, "dve", out_ap_of(off, n), psum[bo][p0:p0 + 64, 0:n], RCP[p0:p0 + 64, 0:n], ALU.mult,
                   [("B", bo), ("RCP", p0)], wout)
                P.free(bo)

        for hd in range(8):
            g, p0, c = hd // 4, 64 * (hd % 2), hd // 2
            attend(lambda off, n: QG[p0:p0 + 64, c, off:off + n],
                   lambda kb: KDUP[p0:p0 + 64, g, kb * 128:(kb + 1) * 128],
                   lambda kb: VAUG[:, kb, g, (64 if p0 else 0):(64 if p0 else 0) + 128],
                   64, p0, 64 ** -0.5,
                   lambda off, n: BOUT[p0:p0 + 64, c, off:off + n],
                   [("QG", ti) for ti in range(ntile)], [("KDUP", "a"), ("KDUP", "c")], [("VAUG", "a"), ("VAUG", "c")], [("BOUT",)])
        barrier()
        P.stage += 1
        if P.stage >= STOP:
            return

        f3 = Frame(nc, st0 if G.sample else fpt.cur, lim, f"L{l}{gn}p3")
        KH = [f3.alloc(f"KH{i}", [128, SK], BF16) for i in range(2)]
        VH = [f3.alloc(f"VH{i}", [128, nkb, 128], BF16) for i in range(2)]
        QH = [f3.alloc(f"QH{i}", [128, T], BF16) for i in range(2)]
        if G.sample:
            assert f3.cur <= st0 + 46080
        MEMSET(P, "pool", VH[0][:, :, 64:128], 1.0, [("VH", 0)])
        MEMSET(P, "pool", VH[1][:, :, 0:64], 1.0, [("VH", 1)])

        def ld_small(slot):
            wload(slot, wview(slot, [128, 2, 768]), wqb_d[l].rearrange("(k p) c -> p k c", p=128))
            P.add("pool", lambda e: e.dma_start(out=WS[slot][:, 1536:2560], in_=wkvb_d[l]), (), [("W", slot)],
                  dma_sem=f"w{slot}", nobar=True, grp=True)
        sM = WSTR.get(ld_small)
        WQB = wview(sM, [128, 2, 768])
        WKVB = WS[sM][:, 1536:2560].rearrange("p (h c) -> p h c", h=8)
        WKN = WS[sM][:, 2560:2560 + 768].rearrange("p (h c) -> p h c", h=8)
        WQP = f3.alloc("WQP", [128, 2, 8, 96], BF16) if G.sample else None
        MEMSET(P, "pool", WKN[:, :, 64:96], 0.0, [("WKN",)])
        CP(P, "pool", WKN[:, :, 0:64], WKVB[:, :, 0:64], [("W", sM)], [("WKN",)])
        if G.sample:
            MEMSET(P, "pool", WQP[:, :, :, 0:64], 0.0, [("WQP",)])
            for k2 in range(2):
                for half in range(2):
                    CP(P, "pool", WQP[:, k2, :, 64:96].rearrange("p h (a f e) -> p h a f e", a=2, f=2)[:, :, :, half, :],
                       WQB[:, k2, :].rearrange("p (h c) -> p h c", h=8)[:, :, 64:96].rearrange(
                           "p h (a f e) -> p h a f e", a=2, f=2)[:, :, :, 1 - half, :], [("W", sM)], [("WQP",)])
        for h in range(8):
            i2 = h % 2
            p0 = 64 * i2
            for c0 in range(0, SK, 512):
                b = P.bank()
                MM(P, psum[b][0:96, 0:512], WKN[:, h, :], CKVT[:, c0:c0 + 512], True, False, [("WKN",), ("CKVT", "a"), ("CKVT", "c")], [("B", b)])
                MM(P, psum[b][0:96, 0:512], SEL[0:32, :], KROPET[0:32, c0:c0 + 512], False, True,
                   [("CONST",), ("KROPET", "a"), ("KROPET", "c")], [("B", b)])
                CP(P, "dve", KH[i2][0:96, c0:c0 + 512], psum[b][0:96, 0:512], [("B", b)], [("KH", i2)])
                P.free(b)
            for k0 in range(0, nkb, 8):
                nb_ = min(8, nkb - k0)
                b = P.bank()
                for i in range(nb_):
                    MM(P, psum[b][:, i * 64:(i + 1) * 64], CKVT[:, (k0 + i) * 128:(k0 + i + 1) * 128], WKVB[:, h, 64:128],
                       True, True, [("CKVT", "a"), ("CKVT", "c"), ("W", sM)], [("B", b)])
                CP(P, "act", VH[i2][:, k0:k0 + nb_, p0:p0 + 64], psum[b][:, 0:nb_ * 64].rearrange("p (b d) -> p b d", d=64),
                   [("B", b)], [("VH", i2)])
                P.free(b)
            for ti, (off, n) in enumerate(G.tiles):
                b = P.bank()
                for k2 in range(2):
                    MM(P, psum[b][0:96, 0:n], WQB[:, k2, h * 96:(h + 1) * 96], QAN[:, k2, off:off + n], k2 == 0, k2 == 1,
                       [("W", sM), ("QAN", ti)], [("B", b)])
                if G.sample:
                    b2 = P.bank()
                    for k2 in range(2):
                        MM(P, psum[b2][0:96, 0:n], WQP[:, k2, h, :], QAN[:, k2, off:off + n], k2 == 0, k2 == 1,
                           [("WQP",), ("QAN", ti)], [("B", b2)])
                    CP(P, "act", QH[i2][0:64, off:off + n], psum[b][0:64, 0:n], [("B", b)], [("QH", i2)])
                    TT(P, "dve", RCP[64:96, 0:n], psum[b][64:96, 0:n], ROPE32[64:96, 0, off:off + n], ALU.mult,
                       [("B", b), ("ROPE32",)], [("RCP", 64)])
                    TT(P, "dve", PT[0][64:96, 0:n].bitcast(BF16), psum[b2][64:96, 0:n], ROPE32[64:96, 1, off:off + n], ALU.mult,
                       [("B", b2), ("ROPE32",)], [("PT", 0)]) if False else None
                    TT(P, "dve", QH[i2][64:96, off:off + n], psum[b2][64:96, 0:n], ROPE32[64:96, 1, off:off + n], ALU.mult,
                       [("B", b2), ("ROPE32",)], [("QH", i2)])
                    TT(P, "dve", QH[i2][64:96, off:off + n], QH[i2][64:96, off:off + n], RCP[64:96, 0:n], ALU.add,
                       [("QH", i2), ("RCP", 64)], [("QH", i2)])
                    P.free(b2)
                else:
                    CP(P, "act", QH[i2][0:96, off:off + n], psum[b][0:96, 0:n], [("B", b)], [("QH", i2)])
                P.free(b)
            attend(lambda off, n: QH[i2][0:96, off:off + n],
                   lambda kb: KH[i2][0:96, kb * 128:(kb + 1) * 128],
                   lambda kb: VH[i2][:, kb, :],
                   96, p0, 96 ** -0.5,
                   lambda off, n: AOUT[p0:p0 + 64, h // 2, off:off + n],
                   [("QH", i2)], [("KH", i2)], [("VH", i2)], [("AOUT",)])
        barrier()
        P.stage += 1
        if P.stage >= STOP:
            return

        f4 = Frame(nc, S0, lim, f"L{l}{gn}p4")
        Lp = G.seglen + 16
        PIN = f4.alloc("PIN", [128, G.nseg, Lp], F32)
        PA = f4.alloc("PA", [128, G.nseg, Lp], F32)
        PB = f4.alloc("PB", [128, G.nseg, Lp], F32)
        PTAB = f4.alloc("PTAB", [128, T], F32)
        POOLED = f4.alloc("POOLED", [128, T], BF16)
        if G.sample:
            MASK = f4.alloc("MASK", [128, TS], F32)
            DMA(P, "sp", MASK[:, :], mask_d.get(), ("tab", 3), (), [("MASK",)])
        MEMSET(P, "pool", PIN[:, :, :], 0.0, [("PIN",)])
        sD = WSTR.get(win_block(l, C_PO, 512))
        WD_ = wview(sD, [128, KC, 512])

        def ld_pool(slot):
            wload(slot, wview(slot, [128, 4, 128]), wpool_d[l].rearrange("g c d -> c g d"))
        sPW = WSTR.get(ld_pool)
        WPL = wview(sPW, [128, 4, 128])
        SL = G.seglen
        for gi in range(4):
            DMA(P, "sp", PTAB[:, :], ptab_d[gn][:, gi, :], ("tab", 3), (), [("PTAB",)])
            for ti, (off, n) in enumerate(G.tiles):
                b = proj(G, ti, mk_lhs(lambda kc, gi=gi: WD_[:, kc, gi * 128:(gi + 1) * 128], [("W", sD)]), 128)
                if G.sample:
                    TT(P, "dve", segv(PIN, G, ti, 8), psum[b][:, 0:n], MASK[:, off:off + n], ALU.mult,
                       [("B", b), ("MASK",)], [("PIN",)])
                else:
                    CP(P, "act", segv(PIN, G, ti, 8), psv(psum[b][:, 0:n], G, ti), [("B", b)], [("PIN",)])
                P.free(b)
            shifts = [(-1, 0), (-1, 1), (-2, 2), (-4, 4)][:gi + 1]
            rng = [None] * (gi + 1)
            lo_, hi_ = 8, SL + 8
            for k in range(gi, -1, -1):
                rng[k] = (lo_, hi_)
                lo_, hi_ = lo_ + shifts[k][0], hi_ + shifts[k][1]
            src, srck = PIN, ("PIN",)
            bufs = [(PA, ("PA",)), (PB, ("PB",))]
            for k in range(gi + 1):
                dst, dstk = bufs[k % 2]
                a0, a1 = rng[k]
                s_lo, s_hi = shifts[k]
                TT(P, "dve", dst[:, :, a0:a1], src[:, :, a0 + s_lo:a1 + s_lo], src[:, :, a0 + s_hi:a1 + s_hi], ALU.add,
                   [srck], [dstk])
                src, srck = dst, dstk
            oth, othk = bufs[(gi + 1) % 2]
            TT(P, "dve", oth[:, :, 8:SL + 8], src[:, :, 8:SL + 8], PTAB[:, :].rearrange("p (s l) -> p s l", l=SL), ALU.mult,
               [srck, ("PTAB",)], [othk])
            TT(P, "dve", POOLED[:, :].rearrange("p (s l) -> p s l", l=SL), oth[:, :, 8:SL + 8], PIN[:, :, 8:SL + 8], ALU.subtract,
               [othk, ("PIN",)], [("POOLED",)])
            for ti, (off, n) in enumerate(G.tiles):
                b = P.bank()
                MM(P, psum[b][:, 0:n], WPL[:, gi, :], POOLED[:, off:off + n], True, True, [("W", sPW), ("POOLED",)], [("B", b)])
                ACT(P, COUT[:, gi, off:off + n], psum[b][:, 0:n], AF.Copy, [("B", b), ("VECS",), ("ROPE32",)], [("COUT",)],
                    scale=vec(V_PSC + gi))
                P.free(b)
        barrier()
        P.stage += 1
        if P.stage >= STOP:
            return

        f5 = Frame(nc, S0, lim, f"L{l}{gn}p5")
        MERGED = f5.alloc("MERGED", [128, KC, T], BF16)
        SG = [f5.alloc(f"SG{i}", [128, 512], F32) for i in range(3)]
        MT = [f5.alloc(f"MT{i}", [128, 512], F32) for i in range(3)]
        BR = [AOUT, BOUT, COUT]
        BRK = [("AOUT",), ("BOUT",), ("COUT",)]
        for jc in range(8):
            def ld_gate(slot, jc=jc):
                for b3 in range(3):
                    c0 = C_GT + b3 * 1024 + jc * 128
                    wload(slot, wview(slot, [128, KC, 3, 128])[:, :, b3, :],
                          win_d[l].rearrange("(k p) c -> p k c", p=128)[:, :, c0:c0 + 128])

            def ld_br(slot, jc=jc):
                for bi in range(3):
                    P.add("pool", lambda e, bi=bi: e.dma_start(
                        out=WS[slot][:, bi * 512:(bi + 1) * 512].rearrange("p (k c) -> p k c", k=4),
                        in_=wbr_d[bi][l].rearrange("(k p) c -> p k c", p=128)[:, :, jc * 128:(jc + 1) * 128]),
                        (), [("W", slot)], dma_sem=f"w{slot}", nobar=True, grp=True)
            sG = WSTR.get(ld_gate)
            sR = WSTR.get(ld_br)
            GW = wview(sG, [128, KC, 3, 128])
            BW = WS[sR][:, 0:1536].rearrange("p (b k c) -> p b k c", b=3, k=4)
            for ti, (off, n) in enumerate(G.tiles):
                for bi in range(3):
                    bg = proj(G, ti, mk_lhs(lambda kc, bi=bi: GW[:, kc, bi, :], [("W", sG)]), 128)
                    ACT(P, SG[bi][:, 0:n], psum[bg][:, 0:n], AF.Sigmoid, [("B", bg)], [("SG", bi)])
                    P.free(bg)
                    bb = P.bank()
                    for k4 in range(4):
                        MM(P, psum[bb][:, 0:n], BW[:, bi, k4, :], BR[bi][:, k4, off:off + n], k4 == 0, k4 == 3,
                           [("W", sR), BRK[bi]], [("B", bb)])
                    TT(P, "dve", MT[bi][:, 0:n], psum[bb][:, 0:n], SG[bi][:, 0:n], ALU.mult, [("B", bb), ("SG", bi)], [("MT", bi)])
                    P.free(bb)
                TT(P, "dve", MT[0][:, 0:n], MT[0][:, 0:n], MT[1][:, 0:n], ALU.add, [("MT", 0), ("MT", 1)], [("MT", 0)])
                TT(P, "dve", MERGED[:, jc, off:off + n], MT[0][:, 0:n], MT[2][:, 0:n], ALU.add, [("MT", 0), ("MT", 2)],
                   [("MERGED", ti)])
        for jb in range(2):
            def ld_wo(slot, jb=jb):
                wload(slot, wview(slot, [128, KC, 512]),
                      wout_d[l].rearrange("(k p) c -> p k c", p=128)[:, :, jb * 512:(jb + 1) * 512])
            sO = WSTR.get(ld_wo)
            WO = wview(sO, [128, KC, 512])
            for jj in range(4):
                jo = jb * 4 + jj
                for ti, (off, n) in enumerate(G.tiles):
                    b = P.bank()
                    for kc in range(KC):
                        MM(P, psum[b][:, 0:n], WO[:, kc, jj * 128:(jj + 1) * 128], MERGED[:, kc, off:off + n], kc == 0, kc == KC - 1,
                           [("W", sO), ("MERGED", ti)], [("B", b)])
                    STT(P, "dve", X[:, jo, off:off + n], psum[b][:, 0:n], MOD[:, l, j, 16 + jo:17 + jo], X[:, jo, off:off + n],
                        ALU.mult, ALU.add, [("B", b), ("MODV",), ("X", gn, ti)], [("X", gn, ti)])
                    P.free(b)
        barrier()
        P.stage += 1
        if P.stage >= STOP:
            return

        f6 = Frame(nc, PH0, lim, f"L{l}{gn}p6")
        norm_mod(G, X, f6, SCL[:, l, j, 8:16], MOD[:, l, j, 24:32], l)
        barrier()
        P.stage += 1
        if P.stage >= STOP:
            return
        f6 = Frame(nc, PH0, lim, f"L{l}{gn}p6b")
        ACTB = f6.alloc("ACTB", [128, NFC, T], BF16)
        Lc = G.seglen + 2
        UG = [f6.alloc(f"UG{i}", [128, G.nseg, Lc], F32) for i in range(2)]
        UV = [f6.alloc(f"UV{i}", [128, T], BF16) for i in range(2)]
        TC = [f6.alloc(f"TC{i}", [128, T], F32) for i in range(2)]
        if G.sample:
            MASK = f6.alloc("MASK", [128, TS], F32)
            DMA(P, "sp", MASK[:, :], mask_d.get(), ("tab", 3), (), [("MASK",)])
        for i in range(2):
            MEMSET(P, "pool", UG[i][:, :, :], 0.0, [("UG", i)])
        SL = G.seglen
        for bi in range(6):
            cb = bi * 512
            nb_ = min(512, D_FF - cb)

            def ld_up(slot, c0=cb, nb_=nb_):
                wload(slot, wview(slot, [128, KC, nb_]), wup_d[l].rearrange("(k p) c -> p k c", p=128)[:, :, c0:c0 + nb_])

            def ld_upv(slot, c0=cb, nb_=nb_):
                wload(slot, wview(slot, [128, KC, nb_]),
                      wup_d[l].rearrange("(k p) c -> p k c", p=128)[:, :, D_FF + c0:D_FF + c0 + nb_])
            sU = WSTR.get(ld_up)
            sV_ = WSTR.get(ld_upv)
            WU = wview(sU, [128, KC, nb_])
            WV = wview(sV_, [128, KC, nb_])
            for cc in range(nb_ // 128):
                c = bi * 4 + cc
                i2 = c % 2
                for ti, (off, n) in enumerate(G.tiles):
                    bg = proj(G, ti, mk_lhs(lambda kc, cc=cc: WU[:, kc, cc * 128:(cc + 1) * 128], [("W", sU)]), 128)
                    if G.sample:
                        TT(P, "dve", segv(UG[i2], G, ti, 1), psum[bg][:, 0:n], MASK[:, off:off + n], ALU.mult,
                           [("B", bg), ("MASK",)], [("UG", i2)])
                    else:
                        CP(P, "dve", segv(UG[i2], G, ti, 1), psv(psum[bg][:, 0:n], G, ti), [("B", bg)], [("UG", i2)])
                    P.free(bg)
                    bv = proj(G, ti, mk_lhs(lambda kc, cc=cc: WV[:, kc, cc * 128:(cc + 1) * 128], [("W", sV_)]), 128)
                    CP(P, "act", UV[i2][:, off:off + n], psum[bv][:, 0:n], [("B", bv)], [("UV", i2)])
                    P.free(bv)
                tc3 = TC[i2][:, :].rearrange("p (s l) -> p s l", l=SL)
                ug = UG[i2]
                ACT(P, tc3, ug[:, :, 0:SL], AF.Identity, [("UG", i2), ("VECS",)], [("TC", i2)],
                    bias=vec(V_CB + c), scale=vec(V_CW + c))
                STT(P, "dve", tc3, ug[:, :, 1:SL + 1], vec(V_CW + NFC + c), tc3, ALU.mult, ALU.add,
                    [("UG", i2), ("TC", i2), ("VECS",)], [("TC", i2)])
                STT(P, "dve", tc3, ug[:, :, 2:SL + 2], vec(V_CW + 2 * NFC + c), tc3, ALU.mult, ALU.add,
                    [("UG", i2), ("TC", i2), ("VECS",)], [("TC", i2)])
                ACT(P, TC[i2][:, :], TC[i2][:, :], AF.Silu, [("TC", i2)], [("TC", i2)])
                TT(P, "dve", ACTB[:, c, :], TC[i2][:, :], UV[i2][:, :], ALU.mult, [("TC", i2), ("UV", i2)], [("ACTB", c)])
        for jo in range(8):
            def ld_dn(slot, jo=jo):
                wload(slot, wview(slot, [128, NFC, 128]),
                      wdn_d[l].rearrange("(k p) c -> p k c", p=128)[:, :, jo * 128:(jo + 1) * 128])
            sDn = WSTR.get(ld_dn)
            WDN = wview(sDn, [128, NFC, 128])
            for ti, (off, n) in enumerate(G.tiles):
                b = P.bank()
                for c in range(NFC):
                    MM(P, psum[b][:, 0:n], WDN[:, c, :], ACTB[:, c, off:off + n], c == 0, c == NFC - 1,
                       [("W", sDn), ("ACTB", c)], [("B", b)])
                STT(P, "dve", X[:, jo, off:off + n], psum[b][:, 0:n], MOD[:, l, j, 40 + jo:41 + jo], X[:, jo, off:off + n],
                    ALU.mult, ALU.add, [("B", b), ("MODV",), ("X", gn, ti)], [("X", gn, ti)])
                P.free(b)
        if (not G.sample) and l == 0:
            DMA(P, "sp", xp_scr, X[:, :, :], "xp", [("X", "p", ti) for ti in range(ntile)], [("XPSCR",)])
        barrier()
        P.lastX = getattr(P, "lastX", {})
        P.lastX[gn] = X

    def final_norm(G):
        gn = G.name
        X = XS if G.sample else P.lastX.get("p", XS)
        if P.dry:
            return
        barrier()
        lim = PH_END if G.sample else XP_OFF
        fr = Frame(nc, PH0, lim, f"fin{gn}")
        SQ = fr.alloc("SQ", [128, KC, 512], BF16)
        RSTD = fr.alloc("RSTD", [128, 512], F32)
        YT = [fr.alloc(f"YT{i}", [128, 512], F32) for i in range(2)]
        for ti, (off, n) in enumerate(G.tiles):
            for kc in range(KC):
                ACT(P, SQ[:, kc, 0:n], X[:, kc, off:off + n], AF.Square, [("X", gn, ti)], [("SQ", kc)])
            rms_rstd(G, [SQ[:, kc, 0:n] for kc in range(KC)], ONES[:, :], n, 1.0 / D, RSTD[:, 0:n],
                     [("SQ", kc) for kc in range(KC)], ("RSTD",))
            if G.sample:
                lo, hi = max(off, HALO) - off, min(off + n, HALO + OWN) - off
                so = off + lo - HALO
            else:
                lo, hi, so = 0, n, off
            for kc in range(KC):
                y = YT[kc % 2]
                STT(P, "dve", y[:, 0:hi - lo], X[:, kc, off + lo:off + hi], VECS[:, 0, V_GF + kc:V_GF + kc + 1], RSTD[:, lo:hi],
                    ALU.mult, ALU.mult, [("X", gn, ti), ("RSTD",), ("VECS",)], [("YT", kc % 2)])
                dst = (ys_d if G.sample else yp_d)[:, kc, so:so + hi - lo]
                DMA(P, "sp", dst, y[:, 0:hi - lo], ("out", 6), [("YT", kc % 2)], [])

    P.dry = True
    emit_all()
    P.dry = False
    emit_all()

    dma_names = sorted(P.dma_counts.keys())
    sems = {e: es.enter_context(nc.semaphore(f"s_{e}")) for e in COMPUTE}
    dma_sems = {n_: es.enter_context(nc.semaphore(f"d_{n_}")) for n_ in dma_names}
    with nc.Block() as block:
        P.emit(block, sems, dma_sems)
    es.close()
    P.used_io = used_io
    return nc, P


def _fm(x2d):
    t = x2d.shape[0]
    return np.ascontiguousarray(x2d.reshape(t, KC, 128).transpose(2, 1, 0))


def _cols(v):
    return np.ascontiguousarray(v.reshape(-1, 128).T)


def _partner(n_half, width):
    idx = np.arange(width)
    return np.where((idx % (2 * n_half)) < n_half, idx + n_half, idx - n_half)


def _rope_table(tabs, d_head, npart_map):
    theta = np.float32(10000.0)
    da = d_head // 2
    nf = da // 2
    freqs = (theta ** (-(np.arange(0, da, 2, dtype=np.float32)) / np.float32(da))).astype(np.float32)
    row = (tabs // 64).astype(np.float32)
    col = (tabs % 64).astype(np.float32)
    ang_r = row[None, :] * freqs[:, None]
    ang_c = col[None, :] * freqs[:, None]
    cos = np.zeros((d_head, len(tabs)), np.float32)
    sin = np.zeros((d_head, len(tabs)), np.float32)
    for d in range(d_head):
        ang = (ang_r if d < da else ang_c)[d % nf]
        cos[d] = np.cos(ang).astype(np.float32)
        sg = -1.0 if (d % da) < nf else 1.0
        sin[d] = sg * np.sin(ang).astype(np.float32)
    return cos, sin


def _pool_tab(t_abs, seq_len, valid):
    out = np.zeros((4, len(t_abs)), np.float32)
    for gi, w in enumerate((2, 4, 8, 16)):
        lo = np.clip(t_abs - w // 2, 0, seq_len)
        hi = np.clip(t_abs - w // 2 + w, 0, seq_len)
        cnt = np.maximum(hi - lo, 1).astype(np.float32)
        out[gi] = np.where(valid, np.float32(1.0) / cnt, np.float32(0.0))
    return out


_CACHE = {}


def prep_inputs(x_prompt, x_sample, c, cache_mla_ckv, cache_mla_krope, cache_gqa_k, cache_gqa_v,
           c_ctx, w_ada, b_ada, g_norm_mix, w_in, g_q_a, w_q_b, g_kv_a, w_kv_b, g_q_gqa, g_k_gqa,
           w_pool, pool_scale, w_br_a, w_br_b, w_br_c, w_out, g_norm_ffn, w_up, conv_w, conv_b,
           w_down, g_final):
    f = lambda a: np.ascontiguousarray(np.asarray(a, dtype=np.float32))
    x_prompt, x_sample, c = f(x_prompt), f(x_sample), f(c)
    cache_mla_ckv, cache_mla_krope, cache_gqa_k, cache_gqa_v = f(cache_mla_ckv), f(cache_mla_krope), f(cache_gqa_k), f(cache_gqa_v)
    c_ctx = f(c_ctx)
    shared = {"w_ada": f(w_ada), "w_in": f(w_in), "w_q_b": f(w_q_b), "w_kv_b": f(w_kv_b), "w_pool": f(w_pool),
              "w_br_a": f(w_br_a), "w_br_b": f(w_br_b), "w_br_c": f(w_br_c), "w_out": f(w_out), "w_up": f(w_up),
              "w_down": f(w_down)}
    b_ada, g_norm_mix, g_q_a, g_kv_a = f(b_ada), f(g_norm_mix), f(g_q_a), f(g_kv_a)
    g_q_gqa, g_k_gqa, pool_scale, g_norm_ffn = f(g_q_gqa), f(g_k_gqa), f(pool_scale), f(g_norm_ffn)
    conv_w, conv_b, g_final = f(conv_w), f(conv_b), f(g_final)

    vecs = np.zeros((128, L, NV), np.float32)
    p64 = _partner(16, 64)
    for l in range(L):
        vecs[:, l, V_GMIX:V_GMIX + 8] = _cols(g_norm_mix[l])
        vecs[:, l, V_GFFN:V_GFFN + 8] = _cols(g_norm_ffn[l])
        vecs[:, l, V_BADA:V_BADA + 48] = _cols(b_ada[l])
        vecs[:, l, V_GQA:V_GQA + 2] = _cols(g_q_a[l])
        vecs[:, l, V_GKVA] = g_kv_a[l]
        vecs[:, l, V_GQ] = np.tile(g_q_gqa[l], 2)
        vecs[:, l, V_GQP] = np.tile(g_q_gqa[l][p64], 2)
        vecs[:, l, V_GK] = np.tile(g_k_gqa[l], 2)
        vecs[:, l, V_GKP] = np.tile(g_k_gqa[l][p64], 2)
        vecs[:, l, V_PSC:V_PSC + 4] = _cols(pool_scale[l])
        for k in range(3):
            vecs[:, l, V_CW + k * NFC:V_CW + (k + 1) * NFC] = _cols(conv_w[l, k])
        vecs[:, l, V_CB:V_CB + NFC] = _cols(conv_b[l])
        vecs[:, l, V_GF:V_GF + 8] = _cols(g_final)
    sel = np.zeros((32, 96), np.float32)
    sel[np.arange(32), 64 + np.arange(32)] = 1.0
    tp = np.arange(TP) % 256
    ptab_p = np.ascontiguousarray(np.broadcast_to(_pool_tab(tp, 256, np.ones(TP, bool))[None], (128, 4, TP)))

    in_maps = []
    for core in range(8):
        b, q = core // 4, core % 4
        a = q * OWN
        t_abs = np.arange(a - HALO, a + OWN + HALO)
        valid = (t_abs >= 0) & (t_abs < 4096)
        tcl = np.clip(t_abs, 0, 4095)
        xs = np.zeros((TS, D), np.float32)
        xs[valid] = x_sample[b, t_abs[valid]]
        cos64, sin64 = _rope_table(tcl, 64, None)
        cos32, sin32 = _rope_table(tcl, 32, None)
        rope64 = np.stack([np.tile(cos64, (2, 1)), np.tile(sin64, (2, 1))], axis=1)
        r32 = np.zeros((128, 2, TS), np.float32)
        for base in (0, 64):
            r32[base:base + 32, 0] = cos32
            r32[base:base + 32, 1] = sin32
        mask = np.ascontiguousarray(np.broadcast_to(valid.astype(np.float32)[None], (128, TS)))
        ptab_s = np.ascontiguousarray(np.broadcast_to(_pool_tab(t_abs, 4096, valid)[None], (128, 4, TS)))
        m = {
            "xp": _fm(x_prompt[4 * core:4 * core + 4].reshape(TP, D)),
            "xs": _fm(xs),
            "cvec": np.ascontiguousarray(np.stack([_cols(c_ctx), _cols(c[b])], axis=2)),
            "c_ckv": np.ascontiguousarray(cache_mla_ckv[b].transpose(0, 2, 1)),
            "c_krope": np.ascontiguousarray(cache_mla_krope[b].transpose(0, 2, 1)),
            "c_k": np.ascontiguousarray(cache_gqa_k[b].transpose(0, 2, 3, 1)),
            "c_v": np.ascontiguousarray(cache_gqa_v[b].reshape(L, 4, 128, 128).transpose(0, 2, 1, 3)),
            "vecs": vecs, "rope64": np.ascontiguousarray(rope64), "rope32": r32, "mask": mask,
            "ptab_s": ptab_s, "ptab_p": ptab_p, "sel": sel,
        }
        m.update(shared)
        in_maps.append(m)

    return in_maps


def kernel(**inputs):
    in_maps = prep_inputs(**inputs)
    if "nc" not in _CACHE:
        _CACHE["nc"], _CACHE["P"] = build_program()
    used = _CACHE["P"].used_io
    in_maps = [{k: m[k] for k in used["in"]} for m in in_maps]
    res = run_bass_kernel_spmd(_CACHE["nc"], in_maps, core_ids=list(range(8)))
    return assemble(res.results)


def assemble(results):
    oshape = {"y_p": (128, KC, TP), "y_s": (128, KC, OWN), "o_ckv": (L, 128, TP), "o_krope": (L, 32, TP),
              "o_k": (L, 2, 64, TP), "o_v": (L, 128, 8, 128)}
    R = [{k: (np.asarray(r[k]).reshape(oshape[k]) if k in r else np.zeros(oshape[k], np.float32)) for k in oshape} for r in results]

    y_prompt = np.zeros((32, 256, D), np.float32)
    y_sample = np.zeros((2, 4096, D), np.float32)
    n_ckv = np.zeros((32, L, 256, 128), np.float32)
    n_kr = np.zeros((32, L, 256, 32), np.float32)
    n_k = np.zeros((32, L, 256, 2, 64), np.float32)
    n_v = np.zeros((32, L, 256, 2, 64), np.float32)
    for core in range(len(R)):
        r = R[core]
        b, q = core // 4, core % 4
        y_prompt[4 * core:4 * core + 4] = np.asarray(r["y_p"]).transpose(2, 1, 0).reshape(4, 256, D)
        y_sample[b, q * OWN:(q + 1) * OWN] = np.asarray(r["y_s"]).transpose(2, 1, 0).reshape(OWN, D)
        n_ckv[4 * core:4 * core + 4] = np.asarray(r["o_ckv"]).reshape(L, 128, 4, 256).transpose(2, 0, 3, 1)
        n_kr[4 * core:4 * core + 4] = np.asarray(r["o_krope"]).reshape(L, 32, 4, 256).transpose(2, 0, 3, 1)
        n_k[4 * core:4 * core + 4] = np.asarray(r["o_k"]).reshape(L, 2, 64, 4, 256).transpose(3, 0, 4, 1, 2)
        ov = np.asarray(r["o_v"]).transpose(0, 2, 1, 3).reshape(L, 4, 256, 2, 64)
        n_v[4 * core:4 * core + 4] = ov.transpose(1, 0, 2, 3, 4)
    return (y_prompt, y_sample, n_ckv, n_kr, n_k, n_v)
```
